# Optimizing a Trainium2 kernel written in Bass

```python
import functools
import jax, jax.numpy as jnp
from jax import lax
import numpy as np

D_MODEL = 1024
BATCH = 16
SEQ = 2048
DEPTH = 4
DEC_BATCH = 8
DEC_SEQ = 64
PAST_LEN = 4096

CHUNK = 64
H_A = 8
Q_LORA = 768
KV_LORA = 256
NOPE_DIM = 64
ROPE_DIM = 32
V_DIM = 64
ROPE_BASE = 10000.0
MLA_SCALE = (NOPE_DIM + ROPE_DIM) ** -0.5
MLA_Q_BLOCK = 128
H_B = 8
D_B = 64
LEFT_CHUNKS = 8
BAND_WINDOW = LEFT_CHUNKS * CHUNK
MAX_REL = 128
BAND_SCALE = D_B ** -0.5
D_FF = 2816
ALPHA = (2 * DEPTH) ** 0.25
BETA = (8 * DEPTH) ** -0.25
NORM_EPS = 1e-5
NEG_INF = -1e30
IN_COLS = Q_LORA + KV_LORA + ROPE_DIM + 3 * H_B * D_B + 2 * D_MODEL
IN_SPLITS = (Q_LORA,
             Q_LORA + KV_LORA,
             Q_LORA + KV_LORA + ROPE_DIM,
             Q_LORA + KV_LORA + ROPE_DIM + H_B * D_B,
             Q_LORA + KV_LORA + ROPE_DIM + 2 * H_B * D_B,
             Q_LORA + KV_LORA + ROPE_DIM + 3 * H_B * D_B,
             Q_LORA + KV_LORA + ROPE_DIM + 3 * H_B * D_B + D_MODEL)

kernel_name = "mla_chunkband_macaron_deepnorm_stream"


def layer_norm(x, g, b):
    xf = x.astype(jnp.float32)
    mu = jnp.mean(xf, axis=-1, keepdims=True)
    var = jnp.mean(jnp.square(xf - mu), axis=-1, keepdims=True)
    return ((xf - mu) * lax.rsqrt(var + NORM_EPS) * g + b).astype(x.dtype)


def rms_norm(x, g):
    xf = x.astype(jnp.float32)
    return (xf * lax.rsqrt(jnp.mean(jnp.square(xf), -1, keepdims=True) + NORM_EPS) * g).astype(x.dtype)


def rope(x, pos):
    half = ROPE_DIM // 2
    inv = ROPE_BASE ** (-jnp.arange(half, dtype=jnp.float32) / half)
    ang = pos.astype(jnp.float32)[:, None] * inv[None, :]
    cos = jnp.cos(ang)[None, :, None, :]
    sin = jnp.sin(ang)[None, :, None, :]
    x1 = x[..., :half].astype(jnp.float32)
    x2 = x[..., half:].astype(jnp.float32)
    return jnp.concatenate([x1 * cos - x2 * sin, x2 * cos + x1 * sin], axis=-1).astype(x.dtype)


def swiglu(x, w1, w2):
    a, g = jnp.split(x @ w1, 2, axis=-1)
    return (jax.nn.silu(a) * g) @ w2


def rel_bias_lookup(table, rel):
    return table[:, jnp.clip(rel, -MAX_REL, MAX_REL) + MAX_REL].astype(jnp.float32)


def mla_prompt(q_abs, q_rope, ckv, kr):
    B, S = ckv.shape[:2]
    nb = S // MLA_Q_BLOCK
    key_chunk = jnp.arange(S, dtype=jnp.int32) // CHUNK

    def block(args):
        qa, qr, qpos = args
        s = (jnp.einsum('bqhc,bkc->bhqk', qa, ckv) + jnp.einsum('bqhr,bkr->bhqk', qr, kr)).astype(jnp.float32) * MLA_SCALE
        mask = key_chunk[None, :] <= (qpos // CHUNK)[:, None]
        s = jnp.where(mask[None, None], s, NEG_INF)
        p = jax.nn.softmax(s, axis=-1).astype(ckv.dtype)
        return jnp.einsum('bhqk,bkc->bqhc', p, ckv)

    to_blocks = lambda t: jnp.moveaxis(t.reshape(B, nb, MLA_Q_BLOCK, *t.shape[2:]), 1, 0)
    qpos = jnp.arange(S, dtype=jnp.int32).reshape(nb, MLA_Q_BLOCK)
    o = lax.map(block, (to_blocks(q_abs), to_blocks(q_rope), qpos))
    return jnp.moveaxis(o, 0, 1).reshape(B, S, H_A, KV_LORA)


def mla_sample(q_abs, q_rope, ckv, kr, cache_ckv, cache_kr):
    keys_c = jnp.concatenate([cache_ckv.astype(ckv.dtype), ckv], axis=1)
    keys_r = jnp.concatenate([cache_kr.astype(kr.dtype), kr], axis=1)
    s = (jnp.einsum('bqhc,bkc->bhqk', q_abs, keys_c) + jnp.einsum('bqhr,bkr->bhqk', q_rope, keys_r)).astype(jnp.float32) * MLA_SCALE
    p = jax.nn.softmax(s, axis=-1).astype(ckv.dtype)
    return jnp.einsum('bhqk,bkc->bqhc', p, keys_c)


def band_prompt(qb, kb, vb, rel_bias):
    B, S = qb.shape[:2]
    nc = S // CHUNK
    L = LEFT_CHUNKS

    def band(t):
        tp = jnp.pad(t, ((0, 0), (L * CHUNK, 0), (0, 0), (0, 0))).reshape(B, nc + L, CHUNK, H_B, D_B)
        return jnp.concatenate([tp[:, j:j + nc] for j in range(L + 1)], axis=2)

    k_band, v_band = band(kb), band(vb)
    qc = qb.reshape(B, nc, CHUNK, H_B, D_B)
    slot = jnp.arange((L + 1) * CHUNK, dtype=jnp.int32)
    rel = L * CHUNK + jnp.arange(CHUNK, dtype=jnp.int32)[:, None] - slot[None, :]
    bias = rel_bias_lookup(rel_bias, rel)
    valid = (jnp.arange(nc, dtype=jnp.int32)[:, None] - L + slot[None, :] // CHUNK) >= 0
    s = jnp.einsum('bcqhd,bckhd->bchqk', qc, k_band).astype(jnp.float32) * BAND_SCALE + bias[None, None]
    s = jnp.where(valid[None, :, None, None, :], s, NEG_INF)
    p = jax.nn.softmax(s, axis=-1).astype(qb.dtype)
    return jnp.einsum('bchqk,bckhd->bcqhd', p, v_band).reshape(B, S, H_B * D_B)


def band_sample(qb, kb, vb, rel_bias, cache_k, cache_v):
    B, T = qb.shape[:2]
    R = cache_k.shape[1]
    k = jnp.concatenate([cache_k.astype(kb.dtype), kb], axis=1)
    v = jnp.concatenate([cache_v.astype(vb.dtype), vb], axis=1)
    q_pos = R + jnp.arange(T, dtype=jnp.int32)
    k_pos = jnp.arange(R + T, dtype=jnp.int32)
    bias = rel_bias_lookup(rel_bias, q_pos[:, None] - k_pos[None, :])
    s = jnp.einsum('bqhd,bkhd->bhqk', qb, k).astype(jnp.float32) * BAND_SCALE + bias[None]
    p = jax.nn.softmax(s, axis=-1).astype(qb.dtype)
    return jnp.einsum('bhqk,bkhd->bqhd', p, v).reshape(B, T, H_B * D_B)


def trunk_layer(x, pos, mla_fn, band_fn, w):
    (ln1_g, ln1_b, f1_w1, f1_w2, w_in, q_g, w_uq, kv_g, w_uk, w_uv, rel_bias,
     w_pa, w_pb, w_out, ln2_g, ln2_b, f2_w1, f2_w2, ln3_g, ln3_b) = w
    B, S = x.shape[:2]
    x = layer_norm(ALPHA * x + 0.5 * swiglu(x, f1_w1, f1_w2), ln1_g, ln1_b)
    z = x @ w_in
    q_lat, ckv_raw, kr_raw, qb, kb, vb, g_a, g_b = jnp.split(z, IN_SPLITS, axis=-1)
    q = (rms_norm(q_lat, q_g) @ w_uq).reshape(B, S, H_A, NOPE_DIM + ROPE_DIM)
    q_rope = rope(q[..., NOPE_DIM:], pos)
    q_abs = jnp.einsum('bshn,chn->bshc', q[..., :NOPE_DIM], w_uk.reshape(KV_LORA, H_A, NOPE_DIM))
    ckv = rms_norm(ckv_raw, kv_g)
    kr = rope(kr_raw[:, :, None, :], pos)[:, :, 0, :]
    o_lat = mla_fn(q_abs, q_rope, ckv, kr)
    o_a = jnp.einsum('bshc,chv->bshv', o_lat, w_uv.reshape(KV_LORA, H_A, V_DIM)).reshape(B, S, H_A * V_DIM)
    qb = qb.reshape(B, S, H_B, D_B)
    kb = kb.reshape(B, S, H_B, D_B)
    vb = vb.reshape(B, S, H_B, D_B)
    o_b = band_fn(qb, kb, vb, rel_bias)
    mix = (jax.nn.sigmoid(g_a) * (o_a @ w_pa) + jax.nn.sigmoid(g_b) * (o_b @ w_pb)) @ w_out
    x = layer_norm(ALPHA * x + mix, ln2_g, ln2_b)
    x = layer_norm(ALPHA * x + 0.5 * swiglu(x, f2_w1, f2_w2), ln3_g, ln3_b)
    return x, ckv, kr, kb, vb


def setup_inputs(seed: int = 0) -> dict:
    key = jax.random.key(seed)
    ks = iter(jax.random.split(key, 32))

    def nrm(shape, scale):
        return scale * jax.random.normal(next(ks), shape, jnp.float32)

    def gain(shape):
        return 1.0 + nrm(shape, 0.02)

    L, D = DEPTH, D_MODEL
    band_rows = min(BAND_WINDOW, PAST_LEN)
    return {
        "x_prompt": nrm((BATCH, SEQ, D), 1.0),
        "x_sample": nrm((DEC_BATCH, DEC_SEQ, D), 1.0),
        "cache_mla_ckv": nrm((L, DEC_BATCH, PAST_LEN, KV_LORA), 1.0),
        "cache_mla_krope": nrm((L, DEC_BATCH, PAST_LEN, ROPE_DIM), 1.0),
        "cache_band_k": nrm((L, DEC_BATCH, band_rows, H_B, D_B), 1.0),
        "cache_band_v": nrm((L, DEC_BATCH, band_rows, H_B, D_B), 1.0),
        "ln1_g": gain((L, D)),
        "ln1_b": nrm((L, D), 0.02),
        "ffn1_w1": nrm((L, D, 2 * D_FF), D ** -0.5),
        "ffn1_w2": nrm((L, D_FF, D), BETA * D_FF ** -0.5),
        "w_in": nrm((L, D, IN_COLS), D ** -0.5),
        "mla_q_norm_g": gain((L, Q_LORA)),
        "mla_w_uq": nrm((L, Q_LORA, H_A * (NOPE_DIM + ROPE_DIM)), Q_LORA ** -0.5),
        "mla_kv_norm_g": gain((L, KV_LORA)),
        "mla_w_uk": nrm((L, KV_LORA, H_A * NOPE_DIM), KV_LORA ** -0.5),
        "mla_w_uv": nrm((L, KV_LORA, H_A * V_DIM), KV_LORA ** -0.5),
        "band_rel_bias": nrm((L, H_B, 2 * MAX_REL + 1), 0.5),
        "w_proj_a": nrm((L, H_A * V_DIM, D), (H_A * V_DIM) ** -0.5),
        "w_proj_b": nrm((L, H_B * D_B, D), (H_B * D_B) ** -0.5),
        "w_out": nrm((L, D, D), BETA * D ** -0.5),
        "ln2_g": gain((L, D)),
        "ln2_b": nrm((L, D), 0.02),
        "ffn2_w1": nrm((L, D, 2 * D_FF), D ** -0.5),
        "ffn2_w2": nrm((L, D_FF, D), BETA * D_FF ** -0.5),
        "ln3_g": gain((L, D)),
        "ln3_b": nrm((L, D), 0.02),
    }


def reference(x_prompt, x_sample, cache_mla_ckv, cache_mla_krope, cache_band_k, cache_band_v,
              ln1_g, ln1_b, ffn1_w1, ffn1_w2, w_in, mla_q_norm_g, mla_w_uq, mla_kv_norm_g,
              mla_w_uk, mla_w_uv, band_rel_bias, w_proj_a, w_proj_b, w_out,
              ln2_g, ln2_b, ffn2_w1, ffn2_w2, ln3_g, ln3_b):
    S = x_prompt.shape[1]
    T = x_sample.shape[1]
    past = cache_mla_ckv.shape[2]
    pos_p = jnp.arange(S, dtype=jnp.int32)
    pos_s = past + jnp.arange(T, dtype=jnp.int32)
    band_rows_p = min(BAND_WINDOW, S)
    xp, xs = x_prompt, x_sample
    ckv_p, kr_p, kb_p, vb_p = [], [], [], []
    ckv_s, kr_s, kb_s, vb_s = [], [], [], []
    for l in range(DEPTH):
        w = (ln1_g[l], ln1_b[l], ffn1_w1[l], ffn1_w2[l], w_in[l], mla_q_norm_g[l], mla_w_uq[l],
             mla_kv_norm_g[l], mla_w_uk[l], mla_w_uv[l], band_rel_bias[l], w_proj_a[l], w_proj_b[l],
             w_out[l], ln2_g[l], ln2_b[l], ffn2_w1[l], ffn2_w2[l], ln3_g[l], ln3_b[l])
        xp, c, r, k, v = trunk_layer(xp, pos_p, mla_prompt, band_prompt, w)
        ckv_p.append(c)
        kr_p.append(r)
        kb_p.append(k[:, S - band_rows_p:])
        vb_p.append(v[:, S - band_rows_p:])
        xs, c, r, k, v = trunk_layer(
            xs, pos_s,
            functools.partial(mla_sample, cache_ckv=cache_mla_ckv[l], cache_kr=cache_mla_krope[l]),
            functools.partial(band_sample, cache_k=cache_band_k[l], cache_v=cache_band_v[l]),
            w)
        ckv_s.append(c)
        kr_s.append(r)
        kb_s.append(k)
        vb_s.append(v)
    return (xp, xs,
            jnp.stack(ckv_p), jnp.stack(kr_p), jnp.stack(kb_p), jnp.stack(vb_p),
            jnp.stack(ckv_s), jnp.stack(kr_s), jnp.stack(kb_s), jnp.stack(vb_s))
```

```python
import os
import numpy as np
import concourse.bass as bass
import concourse.mybir as mybir
from concourse.bass_utils import run_bass_kernel_spmd

F32 = mybir.dt.float32
BF16 = mybir.dt.bfloat16
AF = mybir.ActivationFunctionType
ALU = mybir.AluOpType

D = 1024
SEQ = 2048
DEPTH = 4
DEC_SEQ = 64
PAST = 4096
QL, KVL, RD, NOPE, VD = 768, 256, 32, 64, 64
HA = HB = 8
DB = 64
DFF = 2816
INC = 4640
ALPHA = (2 * DEPTH) ** 0.25
EPS = 1e-5
EPS_LN = EPS / (ALPHA * ALPHA)
MLA_SCALE = (NOPE + RD) ** -0.5
BAND_SCALE = DB ** -0.5
NPOS = SEQ + DEC_SEQ
MW = 1408
ENG = ("pe", "act", "dve", "pool", "sp")
SUB = int(os.environ.get("KSUB", "9"))


class T:
    __slots__ = ("name", "w", "rd", "dsem", "dcnt", "excl", "role")

    def __init__(self, name, role=None):
        self.name = name
        self.role = role or name
        self.w = None
        self.rd = {}
        self.dsem = None
        self.dcnt = 0
        self.excl = False


class Rec:
    def __init__(self, nc):
        self.nc = nc
        self.ops = {e: [] for e in ENG}
        self.cnt = {e: 0 for e in ENG}
        self.sem = {e: nc.alloc_semaphore("s_" + e) for e in ENG}
        self.seen = {e: {} for e in ENG}
        self.dtiles = []
        self.sempool = {}
        self.nsem = 5

    def _waits(self, eng, reads, writes, par=False):
        waits = {}
        pes = self.sem["pe"]

        def need(ev):
            if ev is None:
                return
            sm, v = ev
            if eng == "pe" and sm is pes:
                return
            k = id(sm)
            if self.seen[eng].get(k, 0) >= v:
                return
            if k not in waits or waits[k][1] < v:
                waits[k] = (sm, v)

        for t in reads:
            need(t.w)
            if t.excl:
                for ev in t.rd.values():
                    need(ev)
        for t in writes:
            if not (par and t.w is not None and t.dsem is not None and t.w[0] is t.dsem):
                need(t.w)
            for ev in t.rd.values():
                need(ev)
        for k, (sm, v) in waits.items():
            self.seen[eng][k] = v
        return list(waits.values())

    def op(self, eng, fn, reads=(), writes=()):
        w = self._waits(eng, reads, writes)
        self.cnt[eng] += 1
        ev = (self.sem[eng], self.cnt[eng])
        for t in reads:
            t.rd[id(ev[0])] = ev
        for t in writes:
            t.w = ev
            t.rd = {}
        self.ops[eng].append((w, fn, ev, 1))

    def group(self, eng, fns, reads=(), writes=()):
        w = self._waits(eng, reads, writes)
        self.cnt[eng] += 1
        ev = (self.sem[eng], self.cnt[eng])
        for t in reads:
            t.rd[id(ev[0])] = ev
        for t in writes:
            t.w = ev
            t.rd = {}
        n = len(fns)
        for i, fn in enumerate(fns):
            self.ops[eng].append((w if i == 0 else [], fn, ev if i == n - 1 else None, 1))

    def dma(self, q, fn, reads, writes, semtile, par=True):
        w = self._waits(q, reads, writes, par=par)
        ent = self.sempool.get(semtile.role)
        if ent is None:
            ent = self.sempool[semtile.role] = [self.nc.alloc_semaphore("d_" + semtile.role), 0]
            self.nsem += 1
        semtile.dsem = ent[0]
        ent[1] += 16
        semtile.dcnt = ent[1]
        ev = (ent[0], ent[1])
        for t in reads:
            t.rd[id(ev[0])] = ev
        for t in writes:
            t.w = ev
            t.rd = {}
        self.ops[q].append((w, fn, ev, 16))

    def barrier(self):
        for e in ENG:
            w = []
            for x in ENG:
                if x != e and self.cnt[x] > 0:
                    k = id(self.sem[x])
                    if self.seen[e].get(k, 0) < self.cnt[x]:
                        self.seen[e][k] = self.cnt[x]
                        w.append((self.sem[x], self.cnt[x]))
            for (sm_, cnt_) in self.sempool.values():
                k = id(sm_)
                if self.seen[e].get(k, 0) < cnt_:
                    self.seen[e][k] = cnt_
                    w.append((sm_, cnt_))
            if w:
                self.ops[e].append((w, None, None, 0))

    def emit(self, eng, e):
        for (w, fn, ev, amt) in self.ops[eng]:
            for (sm, v) in w:
                e.wait_ge(sm, v)
            if fn is None:
                continue
            ins = fn(e)
            if ev is not None:
                ins.then_inc(ev[0], amt)


class Buf:
    def __init__(self, ap, tiles):
        self.ap = ap
        self.t = tiles


class Builder:
    def __init__(self, n_layers=DEPTH, prompt_seqs=(0, 1), do_sample=True, stop=None):
        self.stop = stop
        self.n_layers = n_layers
        self.prompt_seqs = prompt_seqs
        self.do_sample = do_sample
        nc = bass.Bass("TRN2", target_bir_lowering=False)
        self.nc = nc
        self.R = Rec(nc)
        self.uid = 0
        self.base = (nc.sbuf_base + 63) // 64 * 64
        self.top = nc.sbuf_top
        di = lambda n, s: nc.dram_tensor(n, s, F32, kind="ExternalInput").ap()
        do = lambda n, s: nc.dram_tensor(n, s, F32, kind="ExternalOutput").ap()
        L = DEPTH
        self.I = dict(
            x_prompt=di("x_prompt", [2, SEQ, D]), x_sample=di("x_sample", [1, DEC_SEQ, D]),
            cache_ckv=di("cache_ckv", [L, PAST, KVL]), cache_kr=di("cache_kr", [L, PAST, RD]),
            cache_bk=di("cache_bk", [L, 512, 512]), cache_bv=di("cache_bv", [L, 512, 512]),
            ln1_g=di("ln1_g", [L, D]), ln1_b=di("ln1_b", [L, D]),
            ffn1_w1=di("ffn1_w1", [L, D, 2 * DFF]), ffn1_w2=di("ffn1_w2", [L, DFF, D]),
            w_in=di("w_in", [L, D, INC]), qg=di("qg", [L, QL]), w_uq=di("w_uq", [L, QL, 768]),
            kvg=di("kvg", [L, KVL]), w_uk=di("w_uk", [L, KVL, 512]), w_uv=di("w_uv", [L, KVL, 512]),
            relb=di("relb", [L, HB, 257]), w_pa=di("w_pa", [L, 512, D]), w_pb=di("w_pb", [L, 512, D]),
            w_out=di("w_out", [L, D, D]), ln2_g=di("ln2_g", [L, D]), ln2_b=di("ln2_b", [L, D]),
            ffn2_w1=di("ffn2_w1", [L, D, 2 * DFF]), ffn2_w2=di("ffn2_w2", [L, DFF, D]),
            ln3_g=di("ln3_g", [L, D]), ln3_b=di("ln3_b", [L, D]),
            c_ident=di("c_ident", [128, 128]), c_jrev=di("c_jrev", [128, 128]),
            c_ropeF=di("c_ropeF", [2, 128, NPOS]), c_ropeT=di("c_ropeT", [NPOS, 2, RD]),
            c_bmask=di("c_bmask", [128, MW]), c_bneg=di("c_bneg", [128, MW]),
        )
        self.O = dict(
            y_p=do("y_p", [2, SEQ, D]), y_s=do("y_s", [1, DEC_SEQ, D]),
            ckv_p=do("ckv_p", [L, 2, SEQ, KVL]), kr_p=do("kr_p", [L, 2, SEQ, RD]),
            kb_p=do("kb_p", [L, 2, 512, 512]), vb_p=do("vb_p", [L, 2, 512, 512]),
            ckv_s=do("ckv_s", [L, 1, DEC_SEQ, KVL]), kr_s=do("kr_s", [L, 1, DEC_SEQ, RD]),
            kb_s=do("kb_s", [L, 1, DEC_SEQ, 512]), vb_s=do("vb_s", [L, 1, DEC_SEQ, 512]),
        )
        self.scr = nc.dram_tensor("scr_ext", [HB, 1536], F32, kind="Internal").ap()
        self.scr_mm = nc.dram_tensor("scr_mm", [DEPTH, 128, HB * MW], BF16, kind="Internal").ap()
        self.mm_built = {}
        self.psb = [nc.alloc_psum_tensor("ps%d" % i, [128, 512], F32) for i in range(8)]
        self.pst = [T("ps%d" % i) for i in range(8)]
        for t_ in self.pst:
            t_.excl = True
        self.psi = 0
        self.psa = 0
        self.build()

    def sb(self, off, shape, dt, ntiles=1, name=None):
        self.uid += 1
        nm = "%s_%d" % (name or "b", self.uid)
        nbytes = int(np.prod(shape[1:])) * (4 if dt == F32 else 2)
        assert off % 32 == 0 and off + nbytes <= self.top, (nm, off, nbytes, self.top)
        h = self.nc.alloc_sbuf_tensor_at(nm, list(shape), dt, offset=off)
        return Buf(h, [T(nm + "_%d" % i, role="%s_%d" % (name or "b", i)) for i in range(ntiles)])

    def ps(self):
        i = 4 + self.psi
        self.psi = (self.psi + 1) % 4
        return self.psb[i], self.pst[i]

    def psacc(self):
        i = self.psa
        self.psa = (self.psa + 1) % 4
        return self.psb[i], self.pst[i]

    def layout(self, start, limit, spec):
        off = start
        d = {}
        for name, shape, dt, nt in spec:
            off = (off + 31) // 32 * 32
            d[name] = self.sb(off, shape, dt, ntiles=nt, name=name)
            off += int(np.prod(shape[1:])) * (4 if dt == F32 else 2)
        assert off <= limit, ("layout overflow", [s[0] for s in spec], off, limit)
        return d

    def mm(self, pt, mms, reads):
        n = len(mms)
        fns = []
        for i, (o, l, r) in enumerate(mms):
            fns.append(lambda e, o=o, l=l, r=r, i=i: e.matmul(o, lhsT=l, rhs=r, start=(i == 0), stop=(i == n - 1)))
        self.R.group("pe", fns, reads=reads, writes=[pt])

    def mm_acc(self, pt, o, l, r, start, stop, reads):
        self.R.op("pe", lambda e: e.matmul(o, lhsT=l, rhs=r, start=start, stop=stop), reads=reads, writes=[pt])

    def act(self, out, in_, func, reads, writes, **kw):
        self.R.op("act", lambda e: e.activation(out=out, in_=in_, func=func, **kw), reads=reads, writes=writes)

    def tt(self, eng, out, a, b, op, reads, writes):
        self.R.op(eng, lambda e: e.tensor_tensor(out=out, in0=a, in1=b, op=op), reads=reads, writes=writes)

    def stt(self, out, in0, scalar, in1, op0, op1, reads, writes):
        self.R.op("dve", lambda e: e.scalar_tensor_tensor(out=out, in0=in0, scalar=scalar, in1=in1, op0=op0, op1=op1),
                  reads=reads, writes=writes)

    def cp(self, eng, out, in_, reads, writes):
        if eng == "act":
            self.R.op("act", lambda e: e.copy(out=out, in_=in_), reads=reads, writes=writes)
        else:
            self.R.op(eng, lambda e: e.tensor_copy(out=out, in_=in_), reads=reads, writes=writes)

    def memset(self, eng, ap, val, writes):
        self.R.op(eng, lambda e: e.memset(ap, val), reads=(), writes=writes)

    def load(self, q, out, in_, wt, par=True, slow=False):
        if slow:
            self.R.dma(q, lambda e: e.dma_start(out=out, in_=in_, allow_slow_non_contiguous=True), reads=(), writes=[wt], semtile=wt, par=par)
        else:
            self.R.dma(q, lambda e: e.dma_start(out=out, in_=in_), reads=(), writes=[wt], semtile=wt, par=par)

    def store(self, out, in_, rt):
        self.R.dma("sp", lambda e: e.dma_start(out=out, in_=in_), reads=[rt], writes=(), semtile=rt)

    def wslice(self, src, kc, cols):
        i = self.ring_i
        self.ring_i = (i + 1) % len(self.ring)
        buf, t = self.ring[i]
        view = buf[:, 0:kc * cols].rearrange("p (k n) -> p k n", k=kc)
        self.load("pool", view, src.rearrange("(k p) n -> p k n", p=128), t, par=False)
        return view, t

    def build(self):
        R = self.R
        base = self.base
        off = base
        self.ident = self.sb(off, [128, 128], F32, name="ident"); off += 512
        self.identb = self.sb(off, [128, 128], BF16, name="identb"); off += 256
        self.jrev = self.sb(off, [128, 128], BF16, name="jrev"); off += 256
        self.ones = self.sb(off, [128, 128], BF16, name="ones"); off += 256
        self.par = self.sb(off, [128, 2, 64], F32, ntiles=2, name="par"); off += 512
        self.kvrow = self.sb(off, [128, KVL], F32, name="kvrow"); off += 1024
        self.swk = self.sb(off, [128, 8, 96], BF16, name="swk"); off += 1536
        self.swq = self.sb(off, [128, 6, 4, 96], BF16, name="swq"); off += 4608
        self.small = self.sb(off, [128, 8], F32, name="small"); off += 64
        self.dyn0 = off
        I = self.I
        self.load("sp", self.ident.ap[:], I["c_ident"], self.ident.t[0])
        self.load("pool", self.identb.ap[:], I["c_ident"], self.identb.t[0])
        self.load("pool", self.jrev.ap[:], I["c_jrev"], self.jrev.t[0])
        self.memset("dve", self.ones.ap[:], 1.0, [self.ones.t[0]])
        self.memset("dve", self.swk.ap[:], 0.0, [self.swk.t[0]])
        self.memset("dve", self.swq.ap[:], 0.0, [self.swq.t[0]])
        self.memset("dve", self.small.ap[:, 0:1], EPS, [self.small.t[0]])
        self.memset("dve", self.small.ap[:, 1:2], EPS_LN, [self.small.t[0]])
        for s in self.prompt_seqs:
            self.run_pass(False, s)
        if self.do_sample:
            self.run_pass(True, 0)
        R.barrier()
        nc = self.nc
        with nc.Block() as block:
            @block.tensor
            def _(e):
                R.emit("pe", e)

            @block.scalar
            def _(e):
                R.emit("act", e)

            @block.vector
            def _(e):
                R.emit("dve", e)

            @block.gpsimd
            def _(e):
                R.emit("pool", e)

            @block.sync
            def _(e):
                R.emit("sp", e)

    def run_pass(self, sample, sidx):
        R = self.R
        I, O = self.I, self.O
        NT = DEC_SEQ if sample else SEQ
        TW = DEC_SEQ if sample else 512
        ntile = NT // TW
        ST = 1 if sample else 2
        BW = min(TW, 128)
        nblk = TW // BW
        pos0 = SEQ if sample else 0
        NK = PAST + DEC_SEQ if sample else SEQ
        nkblk = (NK + 127) // 128
        koff = PAST if sample else 0
        self.TW, self.BW, self.nblk = TW, BW, nblk
        R.barrier()
        off = self.dyn0
        x32 = self.sb(off, [128, 8, NT], F32, ntiles=ntile, name="x32"); off += 8 * NT * 4
        nbk = 1024 if not sample else 640
        topa = self.top // 64 * 64
        oa_off = topa - 2 * 8 * NT
        ob_off = topa - 8 * NT
        oa = self.sb(oa_off, [128, 4, NT], BF16, ntiles=ntile, name="oa")
        ob = self.sb(ob_off, [128, 4, NT], BF16, ntiles=ntile, name="ob")
        kv_spec = [("KT", [128, 8, NK], BF16, 8), ("VV", [128, nkblk, 512], BF16, 1)]
        kb_spec = [("KbT", [128, 4, nbk], BF16, 1), ("Vb", [128, nbk // 128, 512], BF16, 1)]
        if sample:
            fixed = self.layout(off, oa_off, kv_spec + kb_spec)
            off = max(b.ap.manual_sbuf_range[1] for b in fixed.values()) if False else off + 8 * NK * 2 + nkblk * 1024 + 8 * nbk + (nbk // 128) * 1024 + 128
        P0 = (off + 63) // 64 * 64
        ring4 = [("ring%d" % i, [128, 4096], BF16, 1) for i in range(8 if sample else 4)]
        ring2 = ring4[:2]
        f3 = [("f32_%d" % i, [128, TW], F32, 1) for i in range(3)]
        keep = [("keep0", [128, TW], F32, 1), ("keep1", [128, TW], F32, 1)]
        b4 = [("b16_%d" % i, [128, TW], BF16, 1) for i in range(4)]
        b3 = b4[:3]
        stg2 = [("stg0", [128, 1024], F32, 1), ("stg1", [128, 1024], F32, 1)]
        LA = self.layout(P0, topa, ring4 + [("xb", [128, 8, ST * TW], BF16, ST), ("h", [128, 22, ST * TW], BF16, 22)] + f3 + keep + b4 + stg2)
        ringB = [("ring%d" % i, [128, 2304], BF16, 1) for i in range(4 if sample else 2)]
        LB = self.layout(P0, oa_off, ([] if sample else kv_spec) + ringB + [("xb", [128, 8, TW], BF16, 1)] + f3[:2] + keep + b4 + [
            ("stg0", [128, 288], F32, 1), ("stg1", [128, 288], F32, 1), ("QT", [128, 8, TW], BF16, 8), ("ckvT", [128, 2, TW], BF16, 1),
            ("qraw", [128, 6, TW], BF16, 1), ("craw", [128, 2, TW], BF16, 1), ("ropec", [128, TW], F32, 1),
            ("ropes", [128, TW], F32, 1), ("ropeT", [128, nblk, 2, RD], F32, 1)])
        if not sample:
            LB.update(self.layout(ob_off, topa, [("ring2", [128, 2304], BF16, 1), ("ring3", [128, 2304], BF16, 1)]))
        mm_spec = [("Mm", [128, 8, MW], BF16, 8)]
        LC1 = self.layout(P0, oa_off, mm_spec + [("bmaskr", [128, MW], BF16, 1), ("bneg", [128, MW], BF16, 1), ("ext", [8, 1536], F32, 1)]
                          + [("mraw%d" % i, [128, MW], F32, 1) for i in range(4)] + [("mrev%d" % i, [128, MW], BF16, 1) for i in range(2)])
        LC2 = self.layout(P0, oa_off, mm_spec + ([] if sample else kb_spec) + ring4[:4] + [("xb", [128, 8, TW], BF16, 1)] + f3 + b4 + stg2 + [
            ("QbT", [128, 4, TW], BF16, 1)])
        LC2["Mm"] = LC1["Mm"]
        LD = self.layout(P0, oa_off, [("bigA", [128, 4096], BF16, 1), ("bigB", [128, 4096], BF16, 1), ("smA", [128, 2048], BF16, 1),
                                      ("smB", [128, 2048], BF16, 1), ("xb", [128, 8, NT], BF16, ntile)] + f3 + keep + b4 + [("mix", [128, 8, NT], BF16, ntile)])
        if sample:
            LP = self.layout(P0, oa_off, ring4[:4] + [("cck", [128, 32, KVL], BF16, 1), ("cckT", [128, 2, PAST], BF16, 1), ("ckr", [128, 32, 96], BF16, 1),
                                                  ("cbk", [128, 4, 512], BF16, 1)])
            for L_ in (LA, LB, LC1, LC2, LD, LP):
                L_.update(fixed)
        cur = {}
        self.cur = cur

        def use(Ld):
            cur.clear()
            cur.update(Ld)
            self.ring = [(Ld[k].ap, Ld[k].t[0]) for k in sorted(Ld) if k.startswith("ring")]
            self.ring_i = 0
            self.f32i = self.b16i = self.stgi = 0
            cur["_f"] = [Ld[k] for k in sorted(Ld) if k.startswith("f32_")]
            cur["_b"] = [Ld[k] for k in sorted(Ld) if k.startswith("b16_")]
            cur["_s"] = [Ld[k] for k in sorted(Ld) if k.startswith("stg")]

        def ftmp():
            b = cur["_f"][self.f32i % len(cur["_f"])]
            self.f32i += 1
            return b.ap, b.t[0]

        def btmp():
            b = cur["_b"][self.b16i % len(cur["_b"])]
            self.b16i += 1
            return b.ap, b.t[0]

        def stage():
            b = cur["_s"][self.stgi % len(cur["_s"])]
            self.stgi += 1
            return b.ap, b.t[0]

        small = self.small
        par, PT = self.par.ap[:, 0, :], self.par.t[0]
        ones, ONT = self.ones.ap, self.ones.t[0]
        ident, IDT = self.ident.ap, self.ident.t[0]
        X = x32.ap
        use(LA)

        def tc(t):
            return slice(t * TW, (t + 1) * TW)

        xin = I["x_sample"][0] if sample else I["x_prompt"][sidx]
        for t in range(ntile):
            for tb in range(nblk):
                r0 = t * TW + tb * BW
                sa, st_ = stage()
                self.load("sp", sa[0:BW, :], xin[r0:r0 + BW, :], st_)
                for half in range(2):
                    pb, pt = self.ps()
                    fns = []
                    for c4 in range(4):
                        c = half * 4 + c4
                        fns.append(lambda e, o=pb[:, c4 * BW:(c4 + 1) * BW], i_=sa[0:BW, c * 128:(c + 1) * 128]:
                                   e.transpose(o, i_, ident[0:BW, 0:BW]))
                    R.group("pe", fns, reads=[st_, IDT], writes=[pt])
                    self.cp("act" if half else "dve", X[:, half * 4:half * 4 + 4, r0:r0 + BW],
                            pb[:, 0:4 * BW].rearrange("p (c n) -> p c n", c=4), [pt], [x32.t[t]])

        def cast_xb(t, slot):
            xb = cur["xb"]
            for hh in range(2):
                self.cp("act" if hh else "dve", xb.ap[:, hh * 4:hh * 4 + 4, slot * TW:(slot + 1) * TW],
                        X[:, hh * 4:hh * 4 + 4, tc(t)], [x32.t[t]], [xb.t[slot]])

        def layer_norm(t, gcol, bcol, pv, pvt):
            p1, p1t = self.psacc()
            p2, p2t = self.psacc()
            for c in range(8):
                ya, yt = btmp()
                self.cp("dve", ya[:, :], X[:, c, tc(t)], [x32.t[t]], [yt])
                qa, qt = btmp()
                self.act(qa[:, :], X[:, c, tc(t)], AF.Square, [x32.t[t]], [qt])
                self.mm_acc(p1t, p1[:, 0:TW], ones[:, :], ya[:, :], c == 0, c == 7, [ONT, yt])
                self.mm_acc(p2t, p2[:, 0:TW], ones[:, :], qa[:, :], c == 0, c == 7, [ONT, qt])
                yield
            va, vt = cur["keep1"].ap, cur["keep1"].t[0]
            self.act(va[:, :], p1[:, 0:TW], AF.Square, [p1t], [vt], scale=1.0 / D)
            self.stt(va[:, :], p2[:, 0:TW], 1.0 / D, va[:, :], ALU.mult, ALU.subtract, [p2t, vt], [vt])
            self.R.op("act", lambda e, va=va: e.activation(out=va[:, :], in_=va[:, :], func=AF.Ln, bias=small.ap[:, 1:2], scale=1.0),
                      reads=[vt, small.t[0]], writes=[vt])
            self.act(p2[:, 0:TW], va[:, :], AF.Exp, [vt], [p2t], scale=-0.5)
            yield
            for c in range(8):
                ua, ut = ftmp()
                self.stt(ua[:, :], p1[:, 0:TW], -1.0 / D, X[:, c, tc(t)], ALU.mult, ALU.add, [p1t, x32.t[t]], [ut])
                self.tt("dve", ua[:, :], ua[:, :], p2[:, 0:TW], ALU.mult, [ut, p2t], [ut])
                ba_ = pv[:, bcol + c:bcol + c + 1]
                sc_ = pv[:, gcol + c:gcol + c + 1]
                oo_ = X[:, c, tc(t)]
                self.R.op("act", lambda e, ua=ua, ba_=ba_, sc_=sc_, oo_=oo_: e.activation(out=oo_, in_=ua[:, :], func=AF.Identity, bias=ba_, scale=sc_),
                          reads=[ut, pvt], writes=[x32.t[t]])
                yield

        def drain(gens):
            for g in gens or []:
                for _ in g:
                    pass

        def ffn(s, w1, w2, gcol, bcol, pv, pvt, inter=None):
            tiles = list(range(s * ST, min((s + 1) * ST, ntile)))
            xb, h = cur["xb"], cur["h"]
            inter = list(inter or [])

            def tick():
                while inter:
                    try:
                        next(inter[0])
                        return
                    except StopIteration:
                        inter.pop(0)
            for i, t in enumerate(tiles):
                cast_xb(t, i)
            for jg in range(6):
                npair = 4 if jg < 5 else 2
                sa, sat = self.wslice(w1[:, jg * 512:jg * 512 + npair * 128], 8, npair * 128)
                sg, sgt = self.wslice(w1[:, DFF + jg * 512:DFF + jg * 512 + npair * 128], 8, npair * 128)
                for jp in range(npair):
                    j = jg * 4 + jp
                    for i, t in enumerate(tiles):
                        pa, pat = self.ps()
                        pg, pgt = self.ps()
                        xs = xb.ap[:, :, i * TW:(i + 1) * TW]
                        self.mm(pat, [(pa[:, 0:TW], sa[:, k, jp * 128:(jp + 1) * 128], xs[:, k, :]) for k in range(8)], [sat, xb.t[i]])
                        self.mm(pgt, [(pg[:, 0:TW], sg[:, k, jp * 128:(jp + 1) * 128], xs[:, k, :]) for k in range(8)], [sgt, xb.t[i]])
                        fa, ft = ftmp()
                        self.act(fa[:, :], pa[:, 0:TW], AF.Silu, [pat], [ft])
                        self.tt("dve", h.ap[:, j, i * TW:(i + 1) * TW], fa[:, :], pg[:, 0:TW], ALU.mult, [ft, pgt], [h.t[j]])
                        tick()
            for mp in range(4):
                sl = []
                for kh in range(2):
                    sl.append(self.wslice(w2[kh * 1408:(kh + 1) * 1408, mp * 256:(mp + 1) * 256], 11, 256))
                for mm_ in range(2):
                    m = mp * 2 + mm_
                    for i, t in enumerate(tiles):
                        po, pot = self.ps()
                        mms = []
                        for kh in range(2):
                            for jj in range(11):
                                mms.append((po[:, 0:TW], sl[kh][0][:, jj, mm_ * 128:(mm_ + 1) * 128], h.ap[:, kh * 11 + jj, i * TW:(i + 1) * TW]))
                        self.mm(pot, mms, [sl[0][1], sl[1][1]] + h.t)
                        self.stt(X[:, m, tc(t)], po[:, 0:TW], 0.5 / ALPHA, X[:, m, tc(t)], ALU.mult, ALU.add, [pot, x32.t[t]], [x32.t[t]])
            drain(inter)
            return [layer_norm(t, gcol, bcol, pv, pvt) for t in tiles]

        carry = []
        for l in range(self.n_layers):
            if self.stop == 'load':
                break
            if sample:
                drain(carry)
                carry = []
                R.barrier()
            par, PT = self.par.ap[:, l % 2, :], self.par.t[l % 2]
            pcols = [("ln1_g", 0, 8), ("ln1_b", 8, 8), ("ln2_g", 16, 8), ("ln2_b", 24, 8), ("ln3_g", 32, 8), ("ln3_b", 40, 8),
                     ("qg", 48, 6), ("kvg", 54, 2)]
            for nm, c0, n in pcols:
                self.load("sp", par[:, c0:c0 + n], I[nm][l].rearrange("(c p) -> p c", p=128), PT, slow=True)
            self.load("sp", self.kvrow.ap[:, :], I["kvg"][l].partition_broadcast(128), self.kvrow.t[0])
            win = I["w_in"][l]

            if sample:
                use(LP)
                cck, cckT, ckr, cbk = cur["cck"], cur["cckT"], cur["ckr"], cur["cbk"]
                KT, VV, KbT, Vb = cur["KT"], cur["VV"], cur["KbT"], cur["Vb"]
                self.load("pool", cck.ap[:, :, :], I["cache_ckv"][l].rearrange("(j p) c -> p j c", p=128), cck.t[0])
                self.memset("dve", ckr.ap[:, :, 0:64], 0.0, [ckr.t[0]])
                self.load("pool", ckr.ap[:, :, 64:96], I["cache_kr"][l].rearrange("(j p) c -> p j c", p=128), ckr.t[0], par=False)
                self.load("pool", cbk.ap[:, :, :], I["cache_bk"][l].rearrange("(j p) c -> p j c", p=128), cbk.t[0])
                self.load("pool", Vb.ap[:, 0:4, :], I["cache_bv"][l].rearrange("(j p) c -> p j c", p=128), Vb.t[0])
                identb, IBT = self.identb.ap, self.identb.t[0]
                for j4 in range(8):
                    for c in range(2):
                        pb, pt = self.ps()
                        pbb = pb[:, :].bitcast(BF16)
                        fns = []
                        for jj in range(4):
                            j = j4 * 4 + jj
                            fns.append(lambda e, o=pbb[:, jj * 128:(jj + 1) * 128], i_=cck.ap[:, j, c * 128:(c + 1) * 128]: e.transpose(o, i_, identb[:, :]))
                        R.group("pe", fns, reads=[cck.t[0], IBT], writes=[pt])
                        self.cp("dve" if c else "act", cckT.ap[:, c, j4 * 512:(j4 + 1) * 512], pbb[:, 0:512], [pt], [cckT.t[0]])
                    pb, pt = self.ps()
                    pbb = pb[:, :].bitcast(BF16)
                    fns = []
                    for jj in range(4):
                        j = j4 * 4 + jj
                        fns.append(lambda e, o=pbb[0:96, jj * 128:(jj + 1) * 128], i_=ckr.ap[:, j, :]: e.transpose(o, i_, identb[:, :]))
                    R.group("pe", fns, reads=[ckr.t[0], IBT], writes=[pt])
                    for hh in range(8):
                        self.cp("dve" if hh % 2 else "act", KT.ap[64:96, hh, j4 * 512:(j4 + 1) * 512], pbb[64:96, 0:512], [pt], [KT.t[hh]])
                for j in range(4):
                    pb, pt = self.ps()
                    pbb = pb[:, :].bitcast(BF16)
                    fns = []
                    for pr in range(4):
                        fns.append(lambda e, o=pbb[:, pr * 128:(pr + 1) * 128], i_=cbk.ap[:, j, pr * 128:(pr + 1) * 128]: e.transpose(o, i_, identb[:, :]))
                    R.group("pe", fns, reads=[cbk.t[0], IBT], writes=[pt])
                    self.cp("dve", KbT.ap[:, :, j * 128:(j + 1) * 128], pbb[:, 0:512].rearrange("p (a n) -> p a n", a=4), [pt], [KbT.t[0]])
                self.memset("dve", KbT.ap[:, :, 576:640], 0.0, [KbT.t[0]])
                self.memset("dve", Vb.ap[:, 4, :], 0.0, [Vb.t[0]])
                suk, sukt = self.wslice(I["w_uk"][l], 2, 512)
                suv, suvt = self.wslice(I["w_uv"][l], 2, 512)
                for hh in range(8):
                    for kt in range(8):
                        pk, pkt = self.ps()
                        self.mm(pkt, [(pk[0:64, :], suk[:, c, hh * 64:(hh + 1) * 64], cckT.ap[:, c, kt * 512:(kt + 1) * 512]) for c in range(2)],
                                [sukt, cckT.t[0]])
                        self.cp("dve" if kt % 2 else "act", KT.ap[0:64, hh, kt * 512:(kt + 1) * 512], pk[0:64, :], [pkt], [KT.t[hh]])
                for j in range(32):
                    pv, pvt = self.ps()
                    self.mm(pvt, [(pv[:, :], cckT.ap[:, c, j * 128:(j + 1) * 128], suv[:, c, :]) for c in range(2)], [suvt, cckT.t[0]])
                    self.cp("dve" if j % 2 else "act", VV.ap[:, j, :], pv[:, :], [pvt], [VV.t[0]])
                R.barrier()

            use(LA)
            for s in range((ntile + ST - 1) // ST):
                carry = ffn(s, I["ffn1_w1"][l], I["ffn1_w2"][l], 0, 8, par, PT, inter=carry)
            drain(carry)
            carry = []
            R.barrier()
            if self.stop == 'A':
                break

            use(LB)
            xb, QT, ckvT, qraw, craw = cur["xb"], cur["QT"], cur["ckvT"], cur["qraw"], cur["craw"]
            qn = qraw
            ropec, ropes, ropeT, KT, VV = cur["ropec"], cur["ropes"], cur["ropeT"], cur["KT"], cur["VV"]
            for t in range(ntile):
                cast_xb(t, 0)
                xs = xb.ap[:, :, 0:TW]
                XBT = xb.t[0]
                cols = slice(pos0 + t * TW, pos0 + (t + 1) * TW)
                self.load("sp", ropec.ap[:, :], I["c_ropeF"][0][:, cols], ropec.t[0])
                self.load("sp", ropes.ap[:, :], I["c_ropeF"][1][:, cols], ropes.t[0])
                self.load("sp", ropeT.ap[0:BW, :, :, :], I["c_ropeT"][cols].rearrange("(b p) a r -> p b a r", p=BW), ropeT.t[0])
                if self.stop == 'B0':
                    break
                pss, psst = self.psacc()
                dq = []
                for c in range(6):
                    if c % 2 == 0:
                        sl_, slt_ = self.wslice(win[:, c * 128:c * 128 + 256], 8, 256)
                    cc = c % 2
                    pq, pqt = self.ps()
                    self.mm(pqt, [(pq[:, 0:TW], sl_[:, k, cc * 128:(cc + 1) * 128], xs[:, k, :]) for k in range(8)], [slt_, XBT])
                    self.cp("dve", qraw.ap[:, c, :], pq[:, 0:TW], [pqt], [qraw.t[0]])
                    qa, qt = btmp()
                    self.act(qa[:, :], pq[:, 0:TW], AF.Square, [pqt], [qt])
                    dq.append(lambda qa=qa, qt=qt, c=c, pss=pss, psst=psst: self.mm_acc(psst, pss[:, 0:TW], ones[:, :], qa[:, :], c == 0, c == 5, [ONT, qt]))
                    if len(dq) > 1:
                        dq.pop(0)()
                sc, sct = self.wslice(win[:, 768:1056], 8, 288)
                swk, SKT = self.swk.ap, self.swk.t[0]
                self.cp("pool", swk[:, :, 64:80], sc[:, :, 272:288], [sct], [SKT])
                self.cp("pool", swk[:, :, 80:96], sc[:, :, 256:272], [sct], [SKT])
                pssc, pssct = self.psacc()
                for c in range(2):
                    pc, pct = self.ps()
                    self.mm(pct, [(pc[:, 0:TW], sc[:, k, c * 128:(c + 1) * 128], xs[:, k, :]) for k in range(8)], [sct, XBT])
                    if c == 0:
                        dq.pop(0)()
                    self.cp("dve", craw.ap[:, c, :], pc[:, 0:TW], [pct], [craw.t[0]])
                    qa, qt = btmp()
                    self.act(qa[:, :], pc[:, 0:TW], AF.Square, [pct], [qt])
                    dq.append(lambda qa=qa, qt=qt, c=c: self.mm_acc(pssct, pssc[:, 0:TW], ones[:, :], qa[:, :], c == 0, c == 1, [ONT, qt]))
                ra, rt = cur["keep0"].ap, cur["keep0"].t[0]
                self.R.op("act", lambda e, ra=ra, pss=pss: e.activation(out=ra[:, :], in_=pss[:, 0:TW], func=AF.Ln, bias=small.ap[:, 0:1], scale=1.0 / QL),
                          reads=[psst, small.t[0]], writes=[rt])
                self.act(ra[:, :], ra[:, :], AF.Exp, [rt], [rt], scale=-0.5)
                for c in range(6):
                    self.stt(qn.ap[:, c, :], qraw.ap[:, c, :], par[:, 48 + c:49 + c], ra[:, :], ALU.mult, ALU.mult, [qraw.t[0], PT, rt], [qn.t[0]])
                pk1, pk1t = self.ps()
                pk2, pk2t = self.ps()
                self.mm(pk1t, [(pk1[0:96, 0:TW], sc[:, k, 192:288], xs[:, k, :]) for k in range(8)], [sct, XBT])
                while dq:
                    dq.pop(0)()
                self.mm(pk2t, [(pk2[0:96, 0:TW], swk[:, k, :], xs[:, k, :]) for k in range(8)], [SKT, XBT])
                rc, rct = cur["keep1"].ap, cur["keep1"].t[0]
                self.R.op("act", lambda e, rc=rc, pssc=pssc: e.activation(out=rc[:, :], in_=pssc[:, 0:TW], func=AF.Ln, bias=small.ap[:, 0:1], scale=1.0 / KVL),
                          reads=[pssct, small.t[0]], writes=[rct])
                self.act(rc[:, :], rc[:, :], AF.Exp, [rct], [rct], scale=-0.5)
                kc0 = koff + t * TW
                ua, ut = ftmp()
                wa, wt = ftmp()
                self.tt("dve", ua[64:96, :], pk1[64:96, 0:TW], ropec.ap[64:96, :], ALU.mult, [pk1t, ropec.t[0]], [ut])
                self.tt("dve", wa[64:96, :], pk2[64:96, 0:TW], ropes.ap[64:96, :], ALU.mult, [pk2t, ropes.t[0]], [wt])
                for hh in range(8):
                    self.tt("dve", KT.ap[64:96, hh, kc0:kc0 + TW], ua[64:96, :], wa[64:96, :], ALU.add, [ut, wt], [KT.t[hh]])
                tokq = []
                for tb in range(nblk):
                    po, pot = self.psacc()
                    self.mm(pot, [(po[0:BW, 0:288], xs[:, k, tb * BW:(tb + 1) * BW], sc[:, k, 0:288]) for k in range(8)], [sct, XBT])
                    sa, st_ = stage()
                    ja, jt = ftmp()
                    ssa, sst = ftmp()
                    self.act(sa[0:BW, 0:256], po[0:BW, 0:256], AF.Square, [pot], [st_])
                    self.R.op("dve", lambda e, sa=sa, ssa=ssa: e.reduce_sum(out=ssa[0:BW, 0:1], in_=sa[0:BW, 0:256], axis=mybir.AxisListType.X),
                              reads=[st_], writes=[sst])
                    self.R.op("act", lambda e, ssa=ssa: e.activation(out=ssa[0:BW, 0:1], in_=ssa[0:BW, 0:1], func=AF.Ln, bias=small.ap[0:BW, 0:1], scale=1.0 / KVL),
                              reads=[sst, small.t[0]], writes=[sst])
                    self.act(ssa[0:BW, 0:1], ssa[0:BW, 0:1], AF.Exp, [sst], [sst], scale=-0.5)
                    self.stt(sa[0:BW, 0:256], po[0:BW, 0:256], ssa[0:BW, 0:1], self.kvrow.ap[0:BW, :], ALU.mult, ALU.mult,
                             [pot, sst, self.kvrow.t[0]], [st_])
                    self.tt("dve", sa[0:BW, 256:288], po[0:BW, 256:288], ropeT.ap[0:BW, tb, 0, :], ALU.mult, [pot, ropeT.t[0]], [st_])
                    self.tt("dve", ja[0:BW, 0:16], po[0:BW, 272:288], ropeT.ap[0:BW, tb, 1, 0:16], ALU.mult, [pot, ropeT.t[0]], [jt])
                    self.tt("dve", ja[0:BW, 16:32], po[0:BW, 256:272], ropeT.ap[0:BW, tb, 1, 16:32], ALU.mult, [pot, ropeT.t[0]], [jt])
                    self.tt("dve", sa[0:BW, 256:288], sa[0:BW, 256:288], ja[0:BW, 0:32], ALU.add, [st_, jt], [st_])
                    r0 = t * TW + tb * BW
                    oc = O["ckv_s"][l, 0] if sample else O["ckv_p"][l, sidx]
                    ok = O["kr_s"][l, 0] if sample else O["kr_p"][l, sidx]
                    self.store(oc[r0:r0 + BW, :], sa[0:BW, 0:256], st_)
                    self.store(ok[r0:r0 + BW, :], sa[0:BW, 256:288], st_)
                kc0 = koff + t * TW
                for hg in range(2):
                    su, sut = self.wslice(I["w_uq"][l][:, hg * 384:(hg + 1) * 384], 6, 384)
                    su4 = su.rearrange("p k (h n) -> p k h n", h=4)
                    swq, SWT = self.swq.ap, self.swq.t[0]
                    self.cp("pool", swq[:, :, :, 64:80], su4[:, :, :, 80:96], [sut], [SWT])
                    self.cp("pool", swq[:, :, :, 80:96], su4[:, :, :, 64:80], [sut], [SWT])
                    for h4 in range(4):
                        hh = hg * 4 + h4
                        p1, p1t = self.ps()
                        p2, p2t = self.ps()
                        self.mm(p1t, [(p1[0:96, 0:TW], su4[:, k, h4, :], qn.ap[:, k, :]) for k in range(6)], [sut, qn.t[0]])
                        self.mm(p2t, [(p2[0:96, 0:TW], swq[:, k, h4, :], qn.ap[:, k, :]) for k in range(6)], [SWT, qn.t[0]])
                        self.cp("act", QT.ap[0:64, hh, :], p1[0:64, 0:TW], [p1t], [QT.t[hh]])
                        ua, ut = ftmp()
                        wa, wt = ftmp()
                        self.tt("dve", ua[64:96, :], p1[64:96, 0:TW], ropec.ap[64:96, :], ALU.mult, [p1t, ropec.t[0]], [ut])
                        self.tt("dve", wa[64:96, :], p2[64:96, 0:TW], ropes.ap[64:96, :], ALU.mult, [p2t, ropes.t[0]], [wt])
                        self.tt("dve", QT.ap[64:96, hh, :], ua[64:96, :], wa[64:96, :], ALU.add, [ut, wt], [QT.t[hh]])
                    if hg == 0:
                        for c in range(2):
                            self.stt(ckvT.ap[:, c, :], craw.ap[:, c, :], par[:, 54 + c:55 + c], rc[:, :], ALU.mult, ALU.mult, [craw.t[0], PT, rct], [ckvT.t[0]])
                suk, sukt = self.wslice(I["w_uk"][l], 2, 512)
                suv, suvt = self.wslice(I["w_uv"][l], 2, 512)
                for hh in range(8):
                    pk, pkt = self.ps()
                    self.mm(pkt, [(pk[0:64, 0:TW], suk[:, c, hh * 64:(hh + 1) * 64], ckvT.ap[:, c, :]) for c in range(2)], [sukt, ckvT.t[0]])
                    self.cp("act" if hh % 2 else "dve", KT.ap[0:64, hh, kc0:kc0 + TW], pk[0:64, 0:TW], [pkt], [KT.t[hh]])
                for tb in range(nblk):
                    pv, pvt = self.ps()
                    self.mm(pvt, [(pv[0:BW, :], ckvT.ap[:, c, tb * BW:(tb + 1) * BW], suv[:, c, :]) for c in range(2)], [suvt, ckvT.t[0]])
                    gb = (kc0 + tb * BW) // 128
                    self.cp("act", VV.ap[0:BW, gb, :], pv[0:BW, :], [pvt], [VV.t[0]])
                if self.stop == 'B5':
                    continue
                qc0 = (koff + t * TW) // 64
                qc1 = (koff + (t + 1) * TW) // 64 - 1
                pend = []
                for hh in range(8):
                    hb, pr = hh % 2, hh // 2
                    po, pot = self.psacc()
                    pm, pmt = self.psacc()
                    blocks = []
                    for j in range(nkblk):
                        kw = min(128, NK - j * 128)
                        if 2 * j > qc1:
                            continue
                        if 2 * j + 1 <= qc0 or kw <= 64:
                            blocks.append((j, kw, 0, False))
                        else:
                            blocks.append((j, kw, (2 * j - qc0) * 64, True))
                    nb_ = len(blocks)
                    for bi, (j, kw, c0, diag) in enumerate(blocks):
                        psc, psct = self.ps()
                        self.mm(psct, [(psc[0:kw, c0:TW], KT.ap[0:96, hh, j * 128:j * 128 + kw], QT.ap[0:96, hh, c0:TW])], [KT.t[hh], QT.t[hh]])
                        pa_, pt_ = btmp()
                        self.act(pa_[0:kw, c0:TW], psc[0:kw, c0:TW], AF.Exp, [psct], [pt_], scale=MLA_SCALE)
                        if diag:
                            self.memset("dve", pa_[64:128, c0:c0 + 64], 0.0, [pt_])

                        def fin(po=po, pot=pot, pm=pm, pmt=pmt, pa_=pa_, pt_=pt_, j=j, kw=kw, c0=c0, bi=bi, nb_=nb_, pr=pr, hb=hb):
                            self.mm_acc(pot, po[:, c0:TW], VV.ap[0:kw, j, pr * 128:(pr + 1) * 128], pa_[0:kw, c0:TW], bi == 0, bi == nb_ - 1, [VV.t[0], pt_])
                            self.mm_acc(pmt, pm[:, c0:TW], ones[0:kw, :], pa_[0:kw, c0:TW], bi == 0, bi == nb_ - 1, [ONT, pt_])
                            if bi == nb_ - 1:
                                rs = slice(hb * 64, hb * 64 + 64)
                                ra, rt = ftmp()
                                self.R.op("dve", lambda e, ra=ra, pm=pm, rs=rs: e.reciprocal(out=ra[rs, :], in_=pm[rs, 0:TW]), reads=[pmt], writes=[rt])
                                self.tt("dve", oa.ap[rs, pr, tc(t)], po[rs, 0:TW], ra[rs, :], ALU.mult, [pot, rt], [oa.t[t]])
                        pend.append(fin)
                        if len(pend) > 3:
                            pend.pop(0)()
                while pend:
                    pend.pop(0)()
            R.barrier()
            if self.stop is not None and self.stop.startswith('B'):
                break

            build_mm = l not in self.mm_built
            use(LC1 if build_mm else LC2)
            Mm = cur["Mm"]
            if build_mm:
                bmaskr, ext, bneg = cur["bmaskr"], cur["ext"], cur["bneg"]
            if not build_mm:
                mmT = self.mm_built[l]
                self.R.dma("sp", lambda e, l=l, Mm=Mm: e.dma_start(out=Mm.ap[:, :, :], in_=self.scr_mm[l].rearrange("p (h n) -> p h n", h=HB)),
                           reads=[mmT], writes=list(Mm.t), semtile=Mm.t[0], par=False)
            if build_mm:
                self.load("pool", bneg.ap[:, :], I["c_bneg"], bneg.t[0])
                self.load("pool", bmaskr.ap[:, :], I["c_bmask"], bmaskr.t[0])
                self.memset("dve", ext.ap[:, :], 0.0, [ext.t[0]])
                self.load("sp", ext.ap[:, 383:640], I["relb"][l], ext.t[0], par=False)
                self.R.op("dve", lambda e: e.tensor_scalar(out=ext.ap[:, 0:383], in0=ext.ap[:, 0:383], scalar1=ext.ap[:, 383:384], scalar2=None, op0=ALU.add),
                          reads=[ext.t[0]], writes=[ext.t[0]])
                self.R.op("dve", lambda e: e.tensor_scalar(out=ext.ap[:, 640:1536], in0=ext.ap[:, 640:1536], scalar1=ext.ap[:, 639:640], scalar2=None, op0=ALU.add),
                          reads=[ext.t[0]], writes=[ext.t[0]])
                scrT = getattr(self, "scrT", None)
                if scrT is None:
                    scrT = self.scrT = T("scr")
                self.R.dma("sp", lambda e: e.dma_start(out=self.scr, in_=ext.ap[:, :]), reads=[ext.t[0]], writes=[scrT], semtile=scrT, par=False)
            jrev, JT = self.jrev.ap, self.jrev.t[0]
            for hh in (range(8) if build_mm else ()):
                mraw = cur["mraw%d" % (hh % 4)]
                mrev = cur["mrev%d" % (hh % 2)]
                src = bass.AP(tensor=self.scr.tensor, offset=hh * 1536, ap=[[1, 128], [1, MW]])
                self.R.dma("sp", lambda e, src=src, mraw=mraw: e.dma_start(out=mraw.ap[:, :], in_=src), reads=[scrT], writes=[mraw.t[0]], semtile=mraw.t[0], par=False)
                self.tt("dve", mraw.ap[:, :], mraw.ap[:, :], bmaskr.ap[:, :], ALU.mult, [mraw.t[0], bmaskr.t[0]], [mraw.t[0]])
                self.stt(mrev.ap[:, :], mraw.ap[:, :], 1.0 / BAND_SCALE, bneg.ap[:, :], ALU.mult, ALU.add, [mraw.t[0], bneg.t[0]], [mrev.t[0]])
                for c3 in range(3):
                    w_ = 512 if c3 < 2 else MW - 1024
                    pj, pjt = self.ps()
                    self.mm(pjt, [(pj[:, 0:w_], jrev[:, :], mrev.ap[:, c3 * 512:c3 * 512 + w_])], [JT, mrev.t[0]])
                    self.cp("act", Mm.ap[:, hh, c3 * 512:c3 * 512 + w_], pj[:, 0:w_], [pjt], [Mm.t[hh]])
            if build_mm:
                mmT = self.mm_built[l] = T("scrmm%d" % l)
                self.R.dma("sp", lambda e, l=l, Mm=Mm: e.dma_start(out=self.scr_mm[l].rearrange("p (h n) -> p h n", h=HB), in_=Mm.ap[:, :, :]),
                           reads=list(Mm.t), writes=[mmT], semtile=mmT, par=False)
            if build_mm:
                R.barrier()
            if self.stop == 'C1':
                break
            use(LC2)
            xb, QbT, KbT, Vb = cur["xb"], cur["QbT"], cur["KbT"], cur["Vb"]
            for t in range(ntile):
                cast_xb(t, 0)
                xs = xb.ap[:, :, 0:TW]
                XBT = xb.t[0]
                kbase = (t % 2) * 512 if not sample else 512
                sq, sqt = self.wslice(win[:, 1056:1568], 8, 512)
                sk, skt = self.wslice(win[:, 1568:2080], 8, 512)
                sv, svt = self.wslice(win[:, 2080:2592], 8, 512)
                for pr in range(4):
                    pq, pqt = self.ps()
                    self.mm(pqt, [(pq[:, 0:TW], sq[:, k, pr * 128:(pr + 1) * 128], xs[:, k, :]) for k in range(8)], [sqt, XBT])
                    self.cp("act", QbT.ap[:, pr, :], pq[:, 0:TW], [pqt], [QbT.t[0]])
                    pk, pkt = self.ps()
                    self.mm(pkt, [(pk[:, 0:TW], sk[:, k, pr * 128:(pr + 1) * 128], xs[:, k, :]) for k in range(8)], [skt, XBT])
                    self.cp("dve", KbT.ap[:, pr, kbase:kbase + TW], pk[:, 0:TW], [pkt], [KbT.t[0]])
                want_out = sample or (t == ntile - 1)
                for tb in range(nblk):
                    pv, pvt = self.ps()
                    self.mm(pvt, [(pv[0:BW, :], xs[:, k, tb * BW:(tb + 1) * BW], sv[:, k, :]) for k in range(8)], [svt, XBT])
                    vblk = (kbase + tb * BW) // 128
                    self.cp("act", Vb.ap[0:BW, vblk, :], pv[0:BW, :], [pvt], [Vb.t[0]])
                    if want_out:
                        sa, st_ = stage()
                        self.cp("dve", sa[0:BW, 0:512], pv[0:BW, :], [pvt], [st_])
                        pk, pkt = self.ps()
                        self.mm(pkt, [(pk[0:BW, :], xs[:, k, tb * BW:(tb + 1) * BW], sk[:, k, :]) for k in range(8)], [skt, XBT])
                        self.cp("dve", sa[0:BW, 512:1024], pk[0:BW, :], [pkt], [st_])
                        r0 = tb * BW
                        ov = O["vb_s"][l, 0] if sample else O["vb_p"][l, sidx]
                        okb = O["kb_s"][l, 0] if sample else O["kb_p"][l, sidx]
                        self.store(ov[r0:r0 + BW, :], sa[0:BW, 0:512], st_)
                        self.store(okb[r0:r0 + BW, :], sa[0:BW, 512:1024], st_)
                if self.stop == 'C2':
                    continue
                pend = []
                for hh in range(8):
                    hb, pr = hh % 2, hh // 2
                    rs = slice(hb * 64, hb * 64 + 64)
                    po, pot = self.psacc()
                    pm, pmt = self.psacc()
                    blocks = []
                    for r in (0, -1, -2, -3, -4, 1, 2, 3):
                        if sample:
                            if r > 0:
                                continue
                            kcol = 512 + 128 * r
                            kw = 64 if r == 0 else 128
                            qlo, qhi = 0, 0
                        else:
                            gb = 4 * t + r
                            if gb < 0:
                                continue
                            kcol = (gb * 128) % 1024
                            kw = 128
                            qlo, qhi = max(0, 2 * r), min(7, 2 * r + 9)
                        blocks.append((r, kcol, kw, qlo * 64, min(TW, (qhi + 1) * 64)))
                    nb_ = len(blocks)
                    for bi, (r, kcol, kw, a0, a1) in enumerate(blocks):
                        psc, psct = self.ps()
                        m0 = a0 - 128 * r + 384
                        self.mm(psct, [(psc[0:kw, a0:a1], KbT.ap[rs, pr, kcol:kcol + kw], QbT.ap[rs, pr, a0:a1]),
                                       (psc[0:kw, a0:a1], self.identb.ap[:, 0:kw], Mm.ap[:, hh, m0:m0 + (a1 - a0)])],
                                [KbT.t[0], QbT.t[0], Mm.t[hh], self.identb.t[0]])
                        pa_, pt_ = btmp()
                        self.act(pa_[0:kw, a0:a1], psc[0:kw, a0:a1], AF.Exp, [psct], [pt_], scale=BAND_SCALE)
                        vblk = kcol // 128

                        def fin(po=po, pot=pot, pm=pm, pmt=pmt, pa_=pa_, pt_=pt_, vblk=vblk, kw=kw, a0=a0, a1=a1, bi=bi, nb_=nb_, pr=pr, rs=rs):
                            self.mm_acc(pot, po[:, a0:a1], Vb.ap[0:kw, vblk, pr * 128:(pr + 1) * 128], pa_[0:kw, a0:a1], bi == 0, bi == nb_ - 1, [Vb.t[0], pt_])
                            self.mm_acc(pmt, pm[:, a0:a1], ones[0:kw, :], pa_[0:kw, a0:a1], bi == 0, bi == nb_ - 1, [ONT, pt_])
                            if bi == nb_ - 1:
                                ra, rt = ftmp()
                                self.R.op("dve", lambda e, ra=ra, pm=pm, rs=rs: e.reciprocal(out=ra[rs, :], in_=pm[rs, 0:TW]), reads=[pmt], writes=[rt])
                                self.tt("dve", ob.ap[rs, pr, tc(t)], po[rs, 0:TW], ra[rs, :], ALU.mult, [pot, rt], [ob.t[t]])
                        pend.append(fin)
                        if len(pend) > 3:
                            pend.pop(0)()
                while pend:
                    pend.pop(0)()
            R.barrier()
            if self.stop in ('C', 'C2'):
                break

            use(LD)
            xb, mix = cur["xb"], cur["mix"]
            big = [(cur["bigA"].ap, cur["bigA"].t[0]), (cur["bigB"].ap, cur["bigB"].t[0])]
            smr = [(cur["smA"].ap, cur["smA"].t[0]), (cur["smB"].ap, cur["smB"].t[0])]
            for t in range(ntile):
                cast_xb(t, t)
            for mp in range(4):
                gbuf, gt = big[mp % 2]
                gv = gbuf[:, 0:4096].rearrange("p (g k n) -> p g k n", g=2, k=8)
                for g in range(2):
                    c0 = 2592 + g * 1024 + mp * 256
                    self.load("pool", gv[:, g, :, :], win[:, c0:c0 + 256].rearrange("(k p) n -> p k n", p=128), gt, par=(g == 1))
                pbuf, ptl = smr[mp % 2]
                pv_ = pbuf[:, 0:2048].rearrange("p (g k n) -> p g k n", g=2, k=4)
                for g, nm in enumerate(("w_pa", "w_pb")):
                    self.load("pool", pv_[:, g, :, :], I[nm][l][:, mp * 256:(mp + 1) * 256].rearrange("(k p) n -> p k n", p=128), ptl, par=(g == 1))
                for mm_ in range(2):
                    m = mp * 2 + mm_
                    ms = slice(mm_ * 128, (mm_ + 1) * 128)
                    for t in range(ntile):
                        xs = xb.ap[:, :, tc(t)]
                        XBT = xb.t[t]
                        pga, pgat = self.ps()
                        pgb, pgbt = self.ps()
                        pA, pAt = self.psacc()
                        pB, pBt = self.psacc()
                        self.mm(pgat, [(pga[:, 0:TW], gv[:, 0, k, ms], xs[:, k, :]) for k in range(8)], [gt, XBT])
                        self.mm(pgbt, [(pgb[:, 0:TW], gv[:, 1, k, ms], xs[:, k, :]) for k in range(8)], [gt, XBT])
                        self.mm(pAt, [(pA[:, 0:TW], pv_[:, 0, k, ms], oa.ap[:, k, tc(t)]) for k in range(4)], [ptl, oa.t[t]])
                        self.mm(pBt, [(pB[:, 0:TW], pv_[:, 1, k, ms], ob.ap[:, k, tc(t)]) for k in range(4)], [ptl, ob.t[t]])
                        ga_, gat_ = ftmp()
                        gb_, gbt_ = ftmp()
                        self.act(ga_[:, :], pga[:, 0:TW], AF.Sigmoid, [pgat], [gat_])
                        self.act(gb_[:, :], pgb[:, 0:TW], AF.Sigmoid, [pgbt], [gbt_])
                        self.tt("dve", ga_[:, :], ga_[:, :], pA[:, 0:TW], ALU.mult, [gat_, pAt], [gat_])
                        self.tt("dve", gb_[:, :], gb_[:, :], pB[:, 0:TW], ALU.mult, [gbt_, pBt], [gbt_])
                        self.tt("dve", mix.ap[:, m, tc(t)], ga_[:, :], gb_[:, :], ALU.add, [gat_, gbt_], [mix.t[t]])
            sos = []
            for half in range(2):
                sbuf_, sot = big[half]
                so = sbuf_[:, 0:4096].rearrange("p (k n) -> p k n", k=8)
                self.load("pool", so, I["w_out"][l][:, half * 512:(half + 1) * 512].rearrange("(k p) n -> p k n", p=128), sot, par=False)
                sos.append((so, sot))
            gens = []

            def tickd():
                while gens:
                    try:
                        next(gens[0])
                        return
                    except StopIteration:
                        gens.pop(0)
            for t in range(ntile):
                for m in range(8):
                    so, sot = sos[m // 4]
                    m4 = m % 4
                    po, pot = self.ps()
                    self.mm(pot, [(po[:, 0:TW], so[:, k, m4 * 128:(m4 + 1) * 128], mix.ap[:, k, tc(t)]) for k in range(8)], [sot, mix.t[t]])
                    self.stt(X[:, m, tc(t)], po[:, 0:TW], 1.0 / ALPHA, X[:, m, tc(t)], ALU.mult, ALU.add, [pot, x32.t[t]], [x32.t[t]])
                    tickd()
                    tickd()
                gens.append(layer_norm(t, 16, 24, par, PT))
            drain(gens)
            R.barrier()
            if self.stop == 'D':
                break

            use(LA)
            for s in range((ntile + ST - 1) // ST):
                carry = ffn(s, I["ffn2_w1"][l], I["ffn2_w2"][l], 32, 40, par, PT, inter=carry)

        drain(carry)
        R.barrier()
        use(LA)
        yout = O["y_s"][0] if sample else O["y_p"][sidx]
        for t in range(ntile):
            for tb in range(nblk):
                r0 = t * TW + tb * BW
                sa, st_ = stage()
                for half in range(2):
                    pb, pt = self.ps()
                    fns = []
                    for c4 in range(4):
                        c = half * 4 + c4
                        fns.append(lambda e, o=pb[0:BW, c4 * 128:(c4 + 1) * 128], i_=X[:, c, r0:r0 + BW]: e.transpose(o, i_, ident[:, :]))
                    R.group("pe", fns, reads=[x32.t[t], IDT], writes=[pt])
                    self.cp("act" if half else "dve", sa[0:BW, half * 512:(half + 1) * 512], pb[0:BW, :], [pt], [st_])
                self.store(yout[r0:r0 + BW, :], sa[0:BW, :], st_)


def _consts():
    ident = np.eye(128, dtype=np.float32)
    jrev = np.ascontiguousarray(ident[::-1])
    half = RD // 2
    inv = (np.float32(10000.0) ** (-np.arange(half, dtype=np.float32) / np.float32(half))).astype(np.float32)
    pos = np.concatenate([np.arange(SEQ), PAST + np.arange(DEC_SEQ)]).astype(np.float32)
    ang = (pos[:, None] * inv[None, :]).astype(np.float32)
    cos = np.cos(ang).astype(np.float32)
    sin = np.sin(ang).astype(np.float32)
    ropeT = np.zeros((NPOS, 2, RD), np.float32)
    ropeT[:, 0, :half] = cos
    ropeT[:, 0, half:] = cos
    ropeT[:, 1, :half] = -sin
    ropeT[:, 1, half:] = sin
    ropeF = np.zeros((2, 128, NPOS), np.float32)
    for p in range(128):
        r = p % 32
        ropeF[0, p] = cos[:, r % 16]
        ropeF[1, p] = (-sin[:, r] if r < 16 else sin[:, r - 16])
    kk = 127 - np.arange(128)[:, None]
    c = np.arange(MW)[None, :] - 384
    cq = np.floor_divide(c, 64)
    kq = kk // 64
    bmask = ((kq >= cq - 8) & (kq <= cq)).astype(np.float32)
    bneg = ((bmask - 1.0) * 240000.0).astype(np.float32)
    return dict(c_ident=ident, c_jrev=jrev, c_ropeF=ropeF, c_ropeT=ropeT, c_bmask=np.ascontiguousarray(bmask), c_bneg=np.ascontiguousarray(bneg))


_CACHE = {}


def _get_builder(key=(DEPTH, (0, 1), True)):
    if key not in _CACHE:
        _CACHE[key] = Builder(n_layers=key[0], prompt_seqs=key[1], do_sample=key[2], stop=(key[3] if len(key) > 3 else None))
    return _CACHE[key]


def kernel(x_prompt, x_sample, cache_mla_ckv, cache_mla_krope, cache_band_k, cache_band_v,
           ln1_g, ln1_b, ffn1_w1, ffn1_w2, w_in, mla_q_norm_g, mla_w_uq, mla_kv_norm_g,
           mla_w_uk, mla_w_uv, band_rel_bias, w_proj_a, w_proj_b, w_out,
           ln2_g, ln2_b, ffn2_w1, ffn2_w2, ln3_g, ln3_b, _dbg=None, _ncores=8, _trace=False):
    f = lambda a: np.ascontiguousarray(np.asarray(a, dtype=np.float32))
    key = _dbg if _dbg is not None else (DEPTH, (0, 1), True)
    B = _get_builder(key)
    consts = _consts()
    shared = dict(ln1_g=f(ln1_g), ln1_b=f(ln1_b), ffn1_w1=f(ffn1_w1), ffn1_w2=f(ffn1_w2), w_in=f(w_in), qg=f(mla_q_norm_g),
                  w_uq=f(mla_w_uq), kvg=f(mla_kv_norm_g), w_uk=f(mla_w_uk), w_uv=f(mla_w_uv), relb=f(band_rel_bias),
                  w_pa=f(w_proj_a), w_pb=f(w_proj_b), w_out=f(w_out), ln2_g=f(ln2_g), ln2_b=f(ln2_b),
                  ffn2_w1=f(ffn2_w1), ffn2_w2=f(ffn2_w2), ln3_g=f(ln3_g), ln3_b=f(ln3_b), **consts)
    xp, xs = f(x_prompt), f(x_sample)
    cc, ck = f(cache_mla_ckv), f(cache_mla_krope)
    cbk, cbv = f(cache_band_k).reshape(DEPTH, 8, 512, 512), f(cache_band_v).reshape(DEPTH, 8, 512, 512)
    in_maps = []
    for c in range(_ncores):
        m = dict(shared)
        m["x_prompt"] = np.ascontiguousarray(xp[2 * c:2 * c + 2])
        m["x_sample"] = np.ascontiguousarray(xs[c:c + 1])
        m["cache_ckv"] = np.ascontiguousarray(cc[:, c])
        m["cache_kr"] = np.ascontiguousarray(ck[:, c])
        m["cache_bk"] = np.ascontiguousarray(cbk[:, c])
        m["cache_bv"] = np.ascontiguousarray(cbv[:, c])
        in_maps.append(m)
    if _trace:
        res = run_bass_kernel_spmd(B.nc, in_maps, core_ids=list(range(_ncores)), trace=True)
        print('EXEC_TIME_NS', res.exec_time_ns)
        return res.results
    res = run_bass_kernel_spmd(B.nc, in_maps, core_ids=list(range(_ncores)))
    r = res.results
    if _ncores < 8:
        return r
    cat = lambda k, ax: np.concatenate([r[c][k] for c in range(8)], axis=ax)
    y_p = cat("y_p", 0)
    y_s = cat("y_s", 0)
    ckv_p = cat("ckv_p", 1)
    kr_p = cat("kr_p", 1)
    kb_p = cat("kb_p", 1).reshape(DEPTH, 16, 512, HB, DB)
    vb_p = cat("vb_p", 1).reshape(DEPTH, 16, 512, HB, DB)
    ckv_s = cat("ckv_s", 1)
    kr_s = cat("kr_s", 1)
    kb_s = cat("kb_s", 1).reshape(DEPTH, 8, DEC_SEQ, HB, DB)
    vb_s = cat("vb_s", 1).reshape(DEPTH, 8, DEC_SEQ, HB, DB)
    return (y_p, y_s, ckv_p, kr_p, kb_p, vb_p, ckv_s, kr_s, kb_s, vb_s)
```

```python
import os
import numpy as np
import concourse.bass as bass
import concourse.mybir as mybir
from concourse.bass_utils import run_bass_kernel_spmd

F32 = mybir.dt.float32
BF16 = mybir.dt.bfloat16
AF = mybir.ActivationFunctionType
ALU = mybir.AluOpType

D = 1024
SEQ = 2048
DEPTH = 4
DEC_SEQ = 64
PAST = 4096
QL, KVL, RD, NOPE, VD = 768, 256, 32, 64, 64
HA = HB = 8
DB = 64
DFF = 2816
INC = 4640
ALPHA = (2 * DEPTH) ** 0.25
EPS = 1e-5
EPS_LN = EPS / (ALPHA * ALPHA)
MLA_SCALE = (NOPE + RD) ** -0.5
BAND_SCALE = DB ** -0.5
NPOS = SEQ + DEC_SEQ
MW = 1408
ENG = ("pe", "act", "dve", "pool", "sp")
SUB = int(os.environ.get("KSUB", "9"))


class T:
    __slots__ = ("name", "w", "rd", "dsem", "dcnt", "excl", "role")

    def __init__(self, name, role=None):
        self.name = name
        self.role = role or name
        self.w = None
        self.rd = {}
        self.dsem = None
        self.dcnt = 0
        self.excl = False


class Rec:
    def __init__(self, nc):
        self.nc = nc
        self.ops = {e: [] for e in ENG}
        self.cnt = {e: 0 for e in ENG}
        self.sem = {e: nc.alloc_semaphore("s_" + e) for e in ENG}
        self.seen = {e: {} for e in ENG}
        self.dtiles = []
        self.sempool = {}
        self.nsem = 5

    def _waits(self, eng, reads, writes, par=False):
        waits = {}
        pes = self.sem["pe"]

        def need(ev):
            if ev is None:
                return
            sm, v = ev
            if eng == "pe" and sm is pes:
                return
            k = id(sm)
            if self.seen[eng].get(k, 0) >= v:
                return
            if k not in waits or waits[k][1] < v:
                waits[k] = (sm, v)

        for t in reads:
            need(t.w)
            if t.excl:
                for ev in t.rd.values():
                    need(ev)
        for t in writes:
            if not (par and t.w is not None and t.dsem is not None and t.w[0] is t.dsem):
                need(t.w)
            for ev in t.rd.values():
                need(ev)
        for k, (sm, v) in waits.items():
            self.seen[eng][k] = v
        return list(waits.values())

    def op(self, eng, fn, reads=(), writes=()):
        w = self._waits(eng, reads, writes)
        self.cnt[eng] += 1
        ev = (self.sem[eng], self.cnt[eng])
        for t in reads:
            t.rd[id(ev[0])] = ev
        for t in writes:
            t.w = ev
            t.rd = {}
        self.ops[eng].append((w, fn, ev, 1))

    def group(self, eng, fns, reads=(), writes=()):
        w = self._waits(eng, reads, writes)
        self.cnt[eng] += 1
        ev = (self.sem[eng], self.cnt[eng])
        for t in reads:
            t.rd[id(ev[0])] = ev
        for t in writes:
            t.w = ev
            t.rd = {}
        n = len(fns)
        for i, fn in enumerate(fns):
            self.ops[eng].append((w if i == 0 else [], fn, ev if i == n - 1 else None, 1))

    def dma(self, q, fn, reads, writes, semtile, par=True):
        w = self._waits(q, reads, writes, par=par)
        ent = self.sempool.get(semtile.role)
        if ent is None:
            ent = self.sempool[semtile.role] = [self.nc.alloc_semaphore("d_" + semtile.role), 0]
            self.nsem += 1
        semtile.dsem = ent[0]
        ent[1] += 16
        semtile.dcnt = ent[1]
        ev = (ent[0], ent[1])
        for t in reads:
            t.rd[id(ev[0])] = ev
        for t in writes:
            t.w = ev
            t.rd = {}
        self.ops[q].append((w, fn, ev, 16))

    def barrier(self):
        for e in ENG:
            w = []
            for x in ENG:
                if x != e and self.cnt[x] > 0:
                    k = id(self.sem[x])
                    if self.seen[e].get(k, 0) < self.cnt[x]:
                        self.seen[e][k] = self.cnt[x]
                        w.append((self.sem[x], self.cnt[x]))
            for (sm_, cnt_) in self.sempool.values():
                k = id(sm_)
                if self.seen[e].get(k, 0) < cnt_:
                    self.seen[e][k] = cnt_
                    w.append((sm_, cnt_))
            if w:
                self.ops[e].append((w, None, None, 0))

    def emit(self, eng, e):
        for (w, fn, ev, amt) in self.ops[eng]:
            for (sm, v) in w:
                e.wait_ge(sm, v)
            if fn is None:
                continue
            ins = fn(e)
            if ev is not None:
                ins.then_inc(ev[0], amt)


class Buf:
    def __init__(self, ap, tiles):
        self.ap = ap
        self.t = tiles


class Builder:
    def __init__(self, n_layers=DEPTH, prompt_seqs=(0, 1), do_sample=True, stop=None):
        self.stop = stop
        self.n_layers = n_layers
        self.prompt_seqs = prompt_seqs
        self.do_sample = do_sample
        nc = bass.Bass("TRN2", target_bir_lowering=False)
        self.nc = nc
        self.R = Rec(nc)
        self.uid = 0
        self.base = (nc.sbuf_base + 63) // 64 * 64
        self.top = nc.sbuf_top
        di = lambda n, s: nc.dram_tensor(n, s, F32, kind="ExternalInput").ap()
        do = lambda n, s: nc.dram_tensor(n, s, F32, kind="ExternalOutput").ap()
        L = DEPTH
        self.I = dict(
            x_prompt=di("x_prompt", [2, SEQ, D]), x_sample=di("x_sample", [1, DEC_SEQ, D]),
            cache_ckv=di("cache_ckv", [L, PAST, KVL]), cache_kr=di("cache_kr", [L, PAST, RD]),
            cache_bk=di("cache_bk", [L, 512, 512]), cache_bv=di("cache_bv", [L, 512, 512]),
            ln1_g=di("ln1_g", [L, D]), ln1_b=di("ln1_b", [L, D]),
            ffn1_w1=di("ffn1_w1", [L, D, 2 * DFF]), ffn1_w2=di("ffn1_w2", [L, DFF, D]),
            w_in=di("w_in", [L, D, INC]), qg=di("qg", [L, QL]), w_uq=di("w_uq", [L, QL, 768]),
            kvg=di("kvg", [L, KVL]), w_uk=di("w_uk", [L, KVL, 512]), w_uv=di("w_uv", [L, KVL, 512]),
            relb=di("relb", [L, HB, 257]), w_pa=di("w_pa", [L, 512, D]), w_pb=di("w_pb", [L, 512, D]),
            w_out=di("w_out", [L, D, D]), ln2_g=di("ln2_g", [L, D]), ln2_b=di("ln2_b", [L, D]),
            ffn2_w1=di("ffn2_w1", [L, D, 2 * DFF]), ffn2_w2=di("ffn2_w2", [L, DFF, D]),
            ln3_g=di("ln3_g", [L, D]), ln3_b=di("ln3_b", [L, D]),
            c_ident=di("c_ident", [128, 128]), c_jrev=di("c_jrev", [128, 128]),
            c_ropeF=di("c_ropeF", [2, 128, NPOS]), c_ropeT=di("c_ropeT", [NPOS, 2, RD]),
            c_bmask=di("c_bmask", [128, MW]), c_bneg=di("c_bneg", [128, MW]),
        )
        self.O = dict(
            y_p=do("y_p", [2, SEQ, D]), y_s=do("y_s", [1, DEC_SEQ, D]),
            ckv_p=do("ckv_p", [L, 2, SEQ, KVL]), kr_p=do("kr_p", [L, 2, SEQ, RD]),
            kb_p=do("kb_p", [L, 2, 512, 512]), vb_p=do("vb_p", [L, 2, 512, 512]),
            ckv_s=do("ckv_s", [L, 1, DEC_SEQ, KVL]), kr_s=do("kr_s", [L, 1, DEC_SEQ, RD]),
            kb_s=do("kb_s", [L, 1, DEC_SEQ, 512]), vb_s=do("vb_s", [L, 1, DEC_SEQ, 512]),
        )
        self.scr = nc.dram_tensor("scr_ext", [HB, 1536], F32, kind="Internal").ap()
        self.scr_mm = nc.dram_tensor("scr_mm", [DEPTH, 128, HB * MW], BF16, kind="Internal").ap()
        self.mm_built = {}
        self.psb = [nc.alloc_psum_tensor("ps%d" % i, [128, 512], F32) for i in range(8)]
        self.pst = [T("ps%d" % i) for i in range(8)]
        for t_ in self.pst:
            t_.excl = True
        self.psi = 0
        self.psa = 0
        self.build()

    def sb(self, off, shape, dt, ntiles=1, name=None):
        self.uid += 1
        nm = "%s_%d" % (name or "b", self.uid)
        nbytes = int(np.prod(shape[1:])) * (4 if dt == F32 else 2)
        assert off % 32 == 0 and off + nbytes <= self.top, (nm, off, nbytes, self.top)
        h = self.nc.alloc_sbuf_tensor_at(nm, list(shape), dt, offset=off)
        return Buf(h, [T(nm + "_%d" % i, role="%s_%d" % (name or "b", i)) for i in range(ntiles)])

    def ps(self):
        i = 4 + self.psi
        self.psi = (self.psi + 1) % 4
        return self.psb[i], self.pst[i]

    def psacc(self):
        i = self.psa
        self.psa = (self.psa + 1) % 4
        return self.psb[i], self.pst[i]

    def layout(self, start, limit, spec):
        off = start
        d = {}
        for name, shape, dt, nt in spec:
            off = (off + 31) // 32 * 32
            d[name] = self.sb(off, shape, dt, ntiles=nt, name=name)
            off += int(np.prod(shape[1:])) * (4 if dt == F32 else 2)
        assert off <= limit, ("layout overflow", [s[0] for s in spec], off, limit)
        return d

    def mm(self, pt, mms, reads):
        n = len(mms)
        fns = []
        for i, (o, l, r) in enumerate(mms):
            fns.append(lambda e, o=o, l=l, r=r, i=i: e.matmul(o, lhsT=l, rhs=r, start=(i == 0), stop=(i == n - 1)))
        self.R.group("pe", fns, reads=reads, writes=[pt])

    def mm_acc(self, pt, o, l, r, start, stop, reads):
        self.R.op("pe", lambda e: e.matmul(o, lhsT=l, rhs=r, start=start, stop=stop), reads=reads, writes=[pt])

    def act(self, out, in_, func, reads, writes, **kw):
        self.R.op("act", lambda e: e.activation(out=out, in_=in_, func=func, **kw), reads=reads, writes=writes)

    def tt(self, eng, out, a, b, op, reads, writes):
        self.R.op(eng, lambda e: e.tensor_tensor(out=out, in0=a, in1=b, op=op), reads=reads, writes=writes)

    def stt(self, out, in0, scalar, in1, op0, op1, reads, writes):
        self.R.op("dve", lambda e: e.scalar_tensor_tensor(out=out, in0=in0, scalar=scalar, in1=in1, op0=op0, op1=op1),
                  reads=reads, writes=writes)

    def cp(self, eng, out, in_, reads, writes):
        if eng == "act":
            self.R.op("act", lambda e: e.copy(out=out, in_=in_), reads=reads, writes=writes)
        else:
            self.R.op(eng, lambda e: e.tensor_copy(out=out, in_=in_), reads=reads, writes=writes)

    def memset(self, eng, ap, val, writes):
        self.R.op(eng, lambda e: e.memset(ap, val), reads=(), writes=writes)

    def load(self, q, out, in_, wt, par=True, slow=False):
        if slow:
            self.R.dma(q, lambda e: e.dma_start(out=out, in_=in_, allow_slow_non_contiguous=True), reads=(), writes=[wt], semtile=wt, par=par)
        else:
            self.R.dma(q, lambda e: e.dma_start(out=out, in_=in_), reads=(), writes=[wt], semtile=wt, par=par)

    def store(self, out, in_, rt):
        self.R.dma("sp", lambda e: e.dma_start(out=out, in_=in_), reads=[rt], writes=(), semtile=rt)

    def wslice(self, src, kc, cols):
        i = self.ring_i
        self.ring_i = (i + 1) % len(self.ring)
        buf, t = self.ring[i]
        view = buf[:, 0:kc * cols].rearrange("p (k n) -> p k n", k=kc)
        self.load("pool", view, src.rearrange("(k p) n -> p k n", p=128), t, par=False)
        return view, t

    def build(self):
        R = self.R
        base = self.base
        off = base
        self.ident = self.sb(off, [128, 128], F32, name="ident"); off += 512
        self.identb = self.sb(off, [128, 128], BF16, name="identb"); off += 256
        self.jrev = self.sb(off, [128, 128], BF16, name="jrev"); off += 256
        self.ones = self.sb(off, [128, 128], BF16, name="ones"); off += 256
        self.par = self.sb(off, [128, 2, 64], F32, ntiles=2, name="par"); off += 512
        self.kvrow = self.sb(off, [128, KVL], F32, name="kvrow"); off += 1024
        self.swk = self.sb(off, [128, 8, 96], BF16, name="swk"); off += 1536
        self.swq = self.sb(off, [128, 6, 4, 96], BF16, name="swq"); off += 4608
        self.small = self.sb(off, [128, 8], F32, name="small"); off += 64
        self.dyn0 = off
        I = self.I
        self.load("sp", self.ident.ap[:], I["c_ident"], self.ident.t[0])
        self.load("pool", self.identb.ap[:], I["c_ident"], self.identb.t[0])
        self.load("pool", self.jrev.ap[:], I["c_jrev"], self.jrev.t[0])
        self.memset("dve", self.ones.ap[:], 1.0, [self.ones.t[0]])
        self.memset("dve", self.swk.ap[:], 0.0, [self.swk.t[0]])
        self.memset("dve", self.swq.ap[:], 0.0, [self.swq.t[0]])
        self.memset("dve", self.small.ap[:, 0:1], EPS, [self.small.t[0]])
        self.memset("dve", self.small.ap[:, 1:2], EPS_LN, [self.small.t[0]])
        for s in self.prompt_seqs:
            self.run_pass(False, s)
        if self.do_sample:
            self.run_pass(True, 0)
        R.barrier()
        nc = self.nc
        with nc.Block() as block:
            @block.tensor
            def _(e):
                R.emit("pe", e)

            @block.scalar
            def _(e):
                R.emit("act", e)

            @block.vector
            def _(e):
                R.emit("dve", e)

            @block.gpsimd
            def _(e):
                R.emit("pool", e)

            @block.sync
            def _(e):
                R.emit("sp", e)

    def run_pass(self, sample, sidx):
        R = self.R
        I, O = self.I, self.O
        NT = DEC_SEQ if sample else SEQ
        TW = DEC_SEQ if sample else 512
        ntile = NT // TW
        ST = 1 if sample else 2
        BW = min(TW, 128)
        nblk = TW // BW
        pos0 = SEQ if sample else 0
        NK = PAST + DEC_SEQ if sample else SEQ
        nkblk = (NK + 127) // 128
        koff = PAST if sample else 0
        self.TW, self.BW, self.nblk = TW, BW, nblk
        R.barrier()
        off = self.dyn0
        x32 = self.sb(off, [128, 8, NT], F32, ntiles=ntile, name="x32"); off += 8 * NT * 4
        nbk = 1024 if not sample else 640
        topa = self.top // 64 * 64
        oa_off = topa - 2 * 8 * NT
        ob_off = topa - 8 * NT
        oa = self.sb(oa_off, [128, 4, NT], BF16, ntiles=ntile, name="oa")
        ob = self.sb(ob_off, [128, 4, NT], BF16, ntiles=ntile, name="ob")
        kv_spec = [("KT", [128, 8, NK], BF16, 8), ("VV", [128, nkblk, 512], BF16, 1)]
        kb_spec = [("KbT", [128, 4, nbk], BF16, 1), ("Vb", [128, nbk // 128, 512], BF16, 1)]
        if sample:
            fixed = self.layout(off, oa_off, kv_spec + kb_spec)
            off = max(b.ap.manual_sbuf_range[1] for b in fixed.values()) if False else off + 8 * NK * 2 + nkblk * 1024 + 8 * nbk + (nbk // 128) * 1024 + 128
        P0 = (off + 63) // 64 * 64
        ring4 = [("ring%d" % i, [128, 4096], BF16, 1) for i in range(8 if sample else 4)]
        ring2 = ring4[:2]
        f3 = [("f32_%d" % i, [128, TW], F32, 1) for i in range(3)]
        keep = [("keep0", [128, TW], F32, 1), ("keep1", [128, TW], F32, 1)]
        b4 = [("b16_%d" % i, [128, TW], BF16, 1) for i in range(4)]
        b3 = b4[:3]
        stg2 = [("stg0", [128, 1024], F32, 1), ("stg1", [128, 1024], F32, 1)]
        LA = self.layout(P0, topa, ring4 + [("xb", [128, 8, ST * TW], BF16, ST), ("h", [128, 22, ST * TW], BF16, 22)] + f3 + keep + b4 + stg2)
        ringB = [("ring%d" % i, [128, 2304], BF16, 1) for i in range(4 if sample else 2)]
        LB = self.layout(P0, oa_off, ([] if sample else kv_spec) + ringB + [("xb", [128, 8, TW], BF16, 1)] + f3[:2] + keep + b4 + [
            ("stg0", [128, 288], F32, 1), ("stg1", [128, 288], F32, 1), ("QT", [128, 8, TW], BF16, 8), ("ckvT", [128, 2, TW], BF16, 1),
            ("qraw", [128, 6, TW], BF16, 1), ("craw", [128, 2, TW], BF16, 1), ("ropec", [128, TW], F32, 1),
            ("ropes", [128, TW], F32, 1), ("ropeT", [128, nblk, 2, RD], F32, 1)])
        if not sample:
            LB.update(self.layout(ob_off, topa, [("ring2", [128, 2304], BF16, 1), ("ring3", [128, 2304], BF16, 1)]))
        mm_spec = [("Mm", [128, 8, MW], BF16, 8)]
        LC1 = self.layout(P0, oa_off, mm_spec + [("bmaskr", [128, MW], BF16, 1), ("bneg", [128, MW], BF16, 1), ("ext", [8, 1536], F32, 1)]
                          + [("mraw%d" % i, [128, MW], F32, 1) for i in range(4)] + [("mrev%d" % i, [128, MW], BF16, 1) for i in range(2)])
        LC2 = self.layout(P0, oa_off, mm_spec + ([] if sample else kb_spec) + ring4[:4] + [("xb", [128, 8, TW], BF16, 1)] + f3 + b4 + stg2 + [
            ("QbT", [128, 4, TW], BF16, 1)])
        LC2["Mm"] = LC1["Mm"]
        LD = self.layout(P0, oa_off, [("bigA", [128, 4096], BF16, 1), ("bigB", [128, 4096], BF16, 1), ("smA", [128, 2048], BF16, 1),
                                      ("smB", [128, 2048], BF16, 1), ("xb", [128, 8, NT], BF16, ntile)] + f3 + keep + b4 + [("mix", [128, 8, NT], BF16, ntile)])
        if sample:
            LP = self.layout(P0, oa_off, ring4[:4] + [("cck", [128, 32, KVL], BF16, 1), ("cckT", [128, 2, PAST], BF16, 1), ("ckr", [128, 32, 96], BF16, 1),
                                                  ("cbk", [128, 4, 512], BF16, 1)])
            for L_ in (LA, LB, LC1, LC2, LD, LP):
                L_.update(fixed)
        cur = {}
        self.cur = cur

        def use(Ld):
            cur.clear()
            cur.update(Ld)
            self.ring = [(Ld[k].ap, Ld[k].t[0]) for k in sorted(Ld) if k.startswith("ring")]
            self.ring_i = 0
            self.f32i = self.b16i = self.stgi = 0
            cur["_f"] = [Ld[k] for k in sorted(Ld) if k.startswith("f32_")]
            cur["_b"] = [Ld[k] for k in sorted(Ld) if k.startswith("b16_")]
            cur["_s"] = [Ld[k] for k in sorted(Ld) if k.startswith("stg")]

        def ftmp():
            b = cur["_f"][self.f32i % len(cur["_f"])]
            self.f32i += 1
            return b.ap, b.t[0]

        def btmp():
            b = cur["_b"][self.b16i % len(cur["_b"])]
            self.b16i += 1
            return b.ap, b.t[0]

        def stage():
            b = cur["_s"][self.stgi % len(cur["_s"])]
            self.stgi += 1
            return b.ap, b.t[0]

        small = self.small
        par, PT = self.par.ap[:, 0, :], self.par.t[0]
        ones, ONT = self.ones.ap, self.ones.t[0]
        ident, IDT = self.ident.ap, self.ident.t[0]
        X = x32.ap
        use(LA)

        def tc(t):
            return slice(t * TW, (t + 1) * TW)

        xin = I["x_sample"][0] if sample else I["x_prompt"][sidx]
        for t in range(ntile):
            for tb in range(nblk):
                r0 = t * TW + tb * BW
                sa, st_ = stage()
                self.load("sp", sa[0:BW, :], xin[r0:r0 + BW, :], st_)
                for half in range(2):
                    pb, pt = self.ps()
                    fns = []
                    for c4 in range(4):
                        c = half * 4 + c4
                        fns.append(lambda e, o=pb[:, c4 * BW:(c4 + 1) * BW], i_=sa[0:BW, c * 128:(c + 1) * 128]:
                                   e.transpose(o, i_, ident[0:BW, 0:BW]))
                    R.group("pe", fns, reads=[st_, IDT], writes=[pt])
                    self.cp("act" if half else "dve", X[:, half * 4:half * 4 + 4, r0:r0 + BW],
                            pb[:, 0:4 * BW].rearrange("p (c n) -> p c n", c=4), [pt], [x32.t[t]])

        def cast_xb(t, slot):
            xb = cur["xb"]
            for hh in range(2):
                self.cp("act" if hh else "dve", xb.ap[:, hh * 4:hh * 4 + 4, slot * TW:(slot + 1) * TW],
                        X[:, hh * 4:hh * 4 + 4, tc(t)], [x32.t[t]], [xb.t[slot]])

        def layer_norm(t, gcol, bcol, pv, pvt):
            p1, p1t = self.psacc()
            p2, p2t = self.psacc()
            for c in range(8):
                ya, yt = btmp()
                self.cp("dve", ya[:, :], X[:, c, tc(t)], [x32.t[t]], [yt])
                qa, qt = btmp()
                self.act(qa[:, :], X[:, c, tc(t)], AF.Square, [x32.t[t]], [qt])
                self.mm_acc(p1t, p1[:, 0:TW], ones[:, :], ya[:, :], c == 0, c == 7, [ONT, yt])
                self.mm_acc(p2t, p2[:, 0:TW], ones[:, :], qa[:, :], c == 0, c == 7, [ONT, qt])
                yield
            va, vt = cur["keep1"].ap, cur["keep1"].t[0]
            self.act(va[:, :], p1[:, 0:TW], AF.Square, [p1t], [vt], scale=1.0 / D)
            self.stt(va[:, :], p2[:, 0:TW], 1.0 / D, va[:, :], ALU.mult, ALU.subtract, [p2t, vt], [vt])
            self.R.op("act", lambda e, va=va: e.activation(out=va[:, :], in_=va[:, :], func=AF.Ln, bias=small.ap[:, 1:2], scale=1.0),
                      reads=[vt, small.t[0]], writes=[vt])
            self.act(p2[:, 0:TW], va[:, :], AF.Exp, [vt], [p2t], scale=-0.5)
            yield
            for c in range(8):
                ua, ut = ftmp()
                self.stt(ua[:, :], p1[:, 0:TW], -1.0 / D, X[:, c, tc(t)], ALU.mult, ALU.add, [p1t, x32.t[t]], [ut])
                self.tt("dve", ua[:, :], ua[:, :], p2[:, 0:TW], ALU.mult, [ut, p2t], [ut])
                ba_ = pv[:, bcol + c:bcol + c + 1]
                sc_ = pv[:, gcol + c:gcol + c + 1]
                oo_ = X[:, c, tc(t)]
                self.R.op("act", lambda e, ua=ua, ba_=ba_, sc_=sc_, oo_=oo_: e.activation(out=oo_, in_=ua[:, :], func=AF.Identity, bias=ba_, scale=sc_),
                          reads=[ut, pvt], writes=[x32.t[t]])
                yield

        def drain(gens):
            for g in gens or []:
                for _ in g:
                    pass

        def ffn(s, w1, w2, gcol, bcol, pv, pvt, inter=None):
            tiles = list(range(s * ST, min((s + 1) * ST, ntile)))
            xb, h = cur["xb"], cur["h"]
            inter = list(inter or [])

            def tick():
                while inter:
                    try:
                        next(inter[0])
                        return
                    except StopIteration:
                        inter.pop(0)
            for i, t in enumerate(tiles):
                cast_xb(t, i)
            for jg in range(6):
                npair = 4 if jg < 5 else 2
                sa, sat = self.wslice(w1[:, jg * 512:jg * 512 + npair * 128], 8, npair * 128)
                sg, sgt = self.wslice(w1[:, DFF + jg * 512:DFF + jg * 512 + npair * 128], 8, npair * 128)
                for jp in range(npair):
                    j = jg * 4 + jp
                    for i, t in enumerate(tiles):
                        pa, pat = self.ps()
                        pg, pgt = self.ps()
                        xs = xb.ap[:, :, i * TW:(i + 1) * TW]
                        self.mm(pat, [(pa[:, 0:TW], sa[:, k, jp * 128:(jp + 1) * 128], xs[:, k, :]) for k in range(8)], [sat, xb.t[i]])
                        self.mm(pgt, [(pg[:, 0:TW], sg[:, k, jp * 128:(jp + 1) * 128], xs[:, k, :]) for k in range(8)], [sgt, xb.t[i]])
                        fa, ft = ftmp()
                        self.act(fa[:, :], pa[:, 0:TW], AF.Silu, [pat], [ft])
                        self.tt("dve", h.ap[:, j, i * TW:(i + 1) * TW], fa[:, :], pg[:, 0:TW], ALU.mult, [ft, pgt], [h.t[j]])
                        tick()
            for mp in range(4):
                sl = []
                for kh in range(2):
                    sl.append(self.wslice(w2[kh * 1408:(kh + 1) * 1408, mp * 256:(mp + 1) * 256], 11, 256))
                for mm_ in range(2):
                    m = mp * 2 + mm_
                    for i, t in enumerate(tiles):
                        po, pot = self.ps()
                        mms = []
                        for kh in range(2):
                            for jj in range(11):
                                mms.append((po[:, 0:TW], sl[kh][0][:, jj, mm_ * 128:(mm_ + 1) * 128], h.ap[:, kh * 11 + jj, i * TW:(i + 1) * TW]))
                        self.mm(pot, mms, [sl[0][1], sl[1][1]] + h.t)
                        self.stt(X[:, m, tc(t)], po[:, 0:TW], 0.5 / ALPHA, X[:, m, tc(t)], ALU.mult, ALU.add, [pot, x32.t[t]], [x32.t[t]])
            drain(inter)
            return [layer_norm(t, gcol, bcol, pv, pvt) for t in tiles]

        carry = []
        for l in range(self.n_layers):
            if self.stop == 'load':
                break
            if sample:
                drain(carry)
                carry = []
                R.barrier()
            par, PT = self.par.ap[:, l % 2, :], self.par.t[l % 2]
            pcols = [("ln1_g", 0, 8), ("ln1_b", 8, 8), ("ln2_g", 16, 8), ("ln2_b", 24, 8), ("ln3_g", 32, 8), ("ln3_b", 40, 8),
                     ("qg", 48, 6), ("kvg", 54, 2)]
            for nm, c0, n in pcols:
                self.load("sp", par[:, c0:c0 + n], I[nm][l].rearrange("(c p) -> p c", p=128), PT, slow=True)
            self.load("sp", self.kvrow.ap[:, :], I["kvg"][l].partition_broadcast(128), self.kvrow.t[0])
            win = I["w_in"][l]

            if sample:
                use(LP)
                cck, cckT, ckr, cbk = cur["cck"], cur["cckT"], cur["ckr"], cur["cbk"]
                KT, VV, KbT, Vb = cur["KT"], cur["VV"], cur["KbT"], cur["Vb"]
                self.load("pool", cck.ap[:, :, :], I["cache_ckv"][l].rearrange("(j p) c -> p j c", p=128), cck.t[0])
                self.memset("dve", ckr.ap[:, :, 0:64], 0.0, [ckr.t[0]])
                self.load("pool", ckr.ap[:, :, 64:96], I["cache_kr"][l].rearrange("(j p) c -> p j c", p=128), ckr.t[0], par=False)
                self.load("pool", cbk.ap[:, :, :], I["cache_bk"][l].rearrange("(j p) c -> p j c", p=128), cbk.t[0])
                self.load("pool", Vb.ap[:, 0:4, :], I["cache_bv"][l].rearrange("(j p) c -> p j c", p=128), Vb.t[0])
                identb, IBT = self.identb.ap, self.identb.t[0]
                for j4 in range(8):
                    for c in range(2):
                        pb, pt = self.ps()
                        pbb = pb[:, :].bitcast(BF16)
                        fns = []
                        for jj in range(4):
                            j = j4 * 4 + jj
                            fns.append(lambda e, o=pbb[:, jj * 128:(jj + 1) * 128], i_=cck.ap[:, j, c * 128:(c + 1) * 128]: e.transpose(o, i_, identb[:, :]))
                        R.group("pe", fns, reads=[cck.t[0], IBT], writes=[pt])
                        self.cp("dve" if c else "act", cckT.ap[:, c, j4 * 512:(j4 + 1) * 512], pbb[:, 0:512], [pt], [cckT.t[0]])
                    pb, pt = self.ps()
                    pbb = pb[:, :].bitcast(BF16)
                    fns = []
                    for jj in range(4):
                        j = j4 * 4 + jj
                        fns.append(lambda e, o=pbb[0:96, jj * 128:(jj + 1) * 128], i_=ckr.ap[:, j, :]: e.transpose(o, i_, identb[:, :]))
                    R.group("pe", fns, reads=[ckr.t[0], IBT], writes=[pt])
                    for hh in range(8):
                        self.cp("dve" if hh % 2 else "act", KT.ap[64:96, hh, j4 * 512:(j4 + 1) * 512], pbb[64:96, 0:512], [pt], [KT.t[hh]])
                for j in range(4):
                    pb, pt = self.ps()
                    pbb = pb[:, :].bitcast(BF16)
                    fns = []
                    for pr in range(4):
                        fns.append(lambda e, o=pbb[:, pr * 128:(pr + 1) * 128], i_=cbk.ap[:, j, pr * 128:(pr + 1) * 128]: e.transpose(o, i_, identb[:, :]))
                    R.group("pe", fns, reads=[cbk.t[0], IBT], writes=[pt])
                    self.cp("dve", KbT.ap[:, :, j * 128:(j + 1) * 128], pbb[:, 0:512].rearrange("p (a n) -> p a n", a=4), [pt], [KbT.t[0]])
                self.memset("dve", KbT.ap[:, :, 576:640], 0.0, [KbT.t[0]])
                self.memset("dve", Vb.ap[:, 4, :], 0.0, [Vb.t[0]])
                suk, sukt = self.wslice(I["w_uk"][l], 2, 512)
                suv, suvt = self.wslice(I["w_uv"][l], 2, 512)
                for hh in range(8):
                    for kt in range(8):
                        pk, pkt = self.ps()
                        self.mm(pkt, [(pk[0:64, :], suk[:, c, hh * 64:(hh + 1) * 64], cckT.ap[:, c, kt * 512:(kt + 1) * 512]) for c in range(2)],
                                [sukt, cckT.t[0]])
                        self.cp("dve" if kt % 2 else "act", KT.ap[0:64, hh, kt * 512:(kt + 1) * 512], pk[0:64, :], [pkt], [KT.t[hh]])
                for j in range(32):
                    pv, pvt = self.ps()
                    self.mm(pvt, [(pv[:, :], cckT.ap[:, c, j * 128:(j + 1) * 128], suv[:, c, :]) for c in range(2)], [suvt, cckT.t[0]])
                    self.cp("dve" if j % 2 else "act", VV.ap[:, j, :], pv[:, :], [pvt], [VV.t[0]])
                R.barrier()

            use(LA)
            for s in range((ntile + ST - 1) // ST):
                carry = ffn(s, I["ffn1_w1"][l], I["ffn1_w2"][l], 0, 8, par, PT, inter=carry)
            drain(carry)
            carry = []
            R.barrier()
            if self.stop == 'A':
                break

            use(LB)
            xb, QT, ckvT, qraw, craw = cur["xb"], cur["QT"], cur["ckvT"], cur["qraw"], cur["craw"]
            qn = qraw
            ropec, ropes, ropeT, KT, VV = cur["ropec"], cur["ropes"], cur["ropeT"], cur["KT"], cur["VV"]
            for t in range(ntile):
                cast_xb(t, 0)
                xs = xb.ap[:, :, 0:TW]
                XBT = xb.t[0]
                cols = slice(pos0 + t * TW, pos0 + (t + 1) * TW)
                self.load("sp", ropec.ap[:, :], I["c_ropeF"][0][:, cols], ropec.t[0])
                self.load("sp", ropes.ap[:, :], I["c_ropeF"][1][:, cols], ropes.t[0])
                self.load("sp", ropeT.ap[0:BW, :, :, :], I["c_ropeT"][cols].rearrange("(b p) a r -> p b a r", p=BW), ropeT.t[0])
                if self.stop == 'B0':
                    break
                pss, psst = self.psacc()
                dq = []
                for c in range(6):
                    if c % 2 == 0:
                        sl_, slt_ = self.wslice(win[:, c * 128:c * 128 + 256], 8, 256)
                    cc = c % 2
                    pq, pqt = self.ps()
                    self.mm(pqt, [(pq[:, 0:TW], sl_[:, k, cc * 128:(cc + 1) * 128], xs[:, k, :]) for k in range(8)], [slt_, XBT])
                    self.cp("dve", qraw.ap[:, c, :], pq[:, 0:TW], [pqt], [qraw.t[0]])
                    qa, qt = btmp()
                    self.act(qa[:, :], pq[:, 0:TW], AF.Square, [pqt], [qt])
                    dq.append(lambda qa=qa, qt=qt, c=c, pss=pss, psst=psst: self.mm_acc(psst, pss[:, 0:TW], ones[:, :], qa[:, :], c == 0, c == 5, [ONT, qt]))
                    if len(dq) > 1:
                        dq.pop(0)()
                sc, sct = self.wslice(win[:, 768:1056], 8, 288)
                swk, SKT = self.swk.ap, self.swk.t[0]
                self.cp("pool", swk[:, :, 64:80], sc[:, :, 272:288], [sct], [SKT])
                self.cp("pool", swk[:, :, 80:96], sc[:, :, 256:272], [sct], [SKT])
                pssc, pssct = self.psacc()
                for c in range(2):
                    pc, pct = self.ps()
                    self.mm(pct, [(pc[:, 0:TW], sc[:, k, c * 128:(c + 1) * 128], xs[:, k, :]) for k in range(8)], [sct, XBT])
                    if c == 0:
                        dq.pop(0)()
                    self.cp("dve", craw.ap[:, c, :], pc[:, 0:TW], [pct], [craw.t[0]])
                    qa, qt = btmp()
                    self.act(qa[:, :], pc[:, 0:TW], AF.Square, [pct], [qt])
                    dq.append(lambda qa=qa, qt=qt, c=c: self.mm_acc(pssct, pssc[:, 0:TW], ones[:, :], qa[:, :], c == 0, c == 1, [ONT, qt]))
                ra, rt = cur["keep0"].ap, cur["keep0"].t[0]
                self.R.op("act", lambda e, ra=ra, pss=pss: e.activation(out=ra[:, :], in_=pss[:, 0:TW], func=AF.Ln, bias=small.ap[:, 0:1], scale=1.0 / QL),
                          reads=[psst, small.t[0]], writes=[rt])
                self.act(ra[:, :], ra[:, :], AF.Exp, [rt], [rt], scale=-0.5)
                for c in range(6):
                    self.stt(qn.ap[:, c, :], qraw.ap[:, c, :], par[:, 48 + c:49 + c], ra[:, :], ALU.mult, ALU.mult, [qraw.t[0], PT, rt], [qn.t[0]])
                pk1, pk1t = self.ps()
                pk2, pk2t = self.ps()
                self.mm(pk1t, [(pk1[0:96, 0:TW], sc[:, k, 192:288], xs[:, k, :]) for k in range(8)], [sct, XBT])
                while dq:
                    dq.pop(0)()
                self.mm(pk2t, [(pk2[0:96, 0:TW], swk[:, k, :], xs[:, k, :]) for k in range(8)], [SKT, XBT])
                rc, rct = cur["keep1"].ap, cur["keep1"].t[0]
                self.R.op("act", lambda e, rc=rc, pssc=pssc: e.activation(out=rc[:, :], in_=pssc[:, 0:TW], func=AF.Ln, bias=small.ap[:, 0:1], scale=1.0 / KVL),
                          reads=[pssct, small.t[0]], writes=[rct])
                self.act(rc[:, :], rc[:, :], AF.Exp, [rct], [rct], scale=-0.5)
                kc0 = koff + t * TW
                ua, ut = ftmp()
                wa, wt = ftmp()
                self.tt("dve", ua[64:96, :], pk1[64:96, 0:TW], ropec.ap[64:96, :], ALU.mult, [pk1t, ropec.t[0]], [ut])
                self.tt("dve", wa[64:96, :], pk2[64:96, 0:TW], ropes.ap[64:96, :], ALU.mult, [pk2t, ropes.t[0]], [wt])
                for hh in range(8):
                    self.tt("dve", KT.ap[64:96, hh, kc0:kc0 + TW], ua[64:96, :], wa[64:96, :], ALU.add, [ut, wt], [KT.t[hh]])
                tokq = []
                for tb in range(nblk):
                    po, pot = self.psacc()
                    self.mm(pot, [(po[0:BW, 0:288], xs[:, k, tb * BW:(tb + 1) * BW], sc[:, k, 0:288]) for k in range(8)], [sct, XBT])
                    sa, st_ = stage()
                    ja, jt = ftmp()
                    ssa, sst = ftmp()
                    self.act(sa[0:BW, 0:256], po[0:BW, 0:256], AF.Square, [pot], [st_])
                    self.R.op("dve", lambda e, sa=sa, ssa=ssa: e.reduce_sum(out=ssa[0:BW, 0:1], in_=sa[0:BW, 0:256], axis=mybir.AxisListType.X),
                              reads=[st_], writes=[sst])
                    self.R.op("act", lambda e, ssa=ssa: e.activation(out=ssa[0:BW, 0:1], in_=ssa[0:BW, 0:1], func=AF.Ln, bias=small.ap[0:BW, 0:1], scale=1.0 / KVL),
                              reads=[sst, small.t[0]], writes=[sst])
                    self.act(ssa[0:BW, 0:1], ssa[0:BW, 0:1], AF.Exp, [sst], [sst], scale=-0.5)
                    self.stt(sa[0:BW, 0:256], po[0:BW, 0:256], ssa[0:BW, 0:1], self.kvrow.ap[0:BW, :], ALU.mult, ALU.mult,
                             [pot, sst, self.kvrow.t[0]], [st_])
                    self.tt("dve", sa[0:BW, 256:288], po[0:BW, 256:288], ropeT.ap[0:BW, tb, 0, :], ALU.mult, [pot, ropeT.t[0]], [st_])
                    self.tt("dve", ja[0:BW, 0:16], po[0:BW, 272:288], ropeT.ap[0:BW, tb, 1, 0:16], ALU.mult, [pot, ropeT.t[0]], [jt])
                    self.tt("dve", ja[0:BW, 16:32], po[0:BW, 256:272], ropeT.ap[0:BW, tb, 1, 16:32], ALU.mult, [pot, ropeT.t[0]], [jt])
                    self.tt("dve", sa[0:BW, 256:288], sa[0:BW, 256:288], ja[0:BW, 0:32], ALU.add, [st_, jt], [st_])
                    r0 = t * TW + tb * BW
                    oc = O["ckv_s"][l, 0] if sample else O["ckv_p"][l, sidx]
                    ok = O["kr_s"][l, 0] if sample else O["kr_p"][l, sidx]
                    self.store(oc[r0:r0 + BW, :], sa[0:BW, 0:256], st_)
                    self.store(ok[r0:r0 + BW, :], sa[0:BW, 256:288], st_)
                kc0 = koff + t * TW
                for hg in range(2):
                    su, sut = self.wslice(I["w_uq"][l][:, hg * 384:(hg + 1) * 384], 6, 384)
                    su4 = su.rearrange("p k (h n) -> p k h n", h=4)
                    swq, SWT = self.swq.ap, self.swq.t[0]
                    self.cp("pool", swq[:, :, :, 64:80], su4[:, :, :, 80:96], [sut], [SWT])
                    self.cp("pool", swq[:, :, :, 80:96], su4[:, :, :, 64:80], [sut], [SWT])
                    for h4 in range(4):
                        hh = hg * 4 + h4
                        p1, p1t = self.ps()
                        p2, p2t = self.ps()
                        self.mm(p1t, [(p1[0:96, 0:TW], su4[:, k, h4, :], qn.ap[:, k, :]) for k in range(6)], [sut, qn.t[0]])
                        self.mm(p2t, [(p2[0:96, 0:TW], swq[:, k, h4, :], qn.ap[:, k, :]) for k in range(6)], [SWT, qn.t[0]])
                        self.cp("act", QT.ap[0:64, hh, :], p1[0:64, 0:TW], [p1t], [QT.t[hh]])
                        ua, ut = ftmp()
                        wa, wt = ftmp()
                        self.tt("dve", ua[64:96, :], p1[64:96, 0:TW], ropec.ap[64:96, :], ALU.mult, [p1t, ropec.t[0]], [ut])
                        self.tt("dve", wa[64:96, :], p2[64:96, 0:TW], ropes.ap[64:96, :], ALU.mult, [p2t, ropes.t[0]], [wt])
                        self.tt("dve", QT.ap[64:96, hh, :], ua[64:96, :], wa[64:96, :], ALU.add, [ut, wt], [QT.t[hh]])
                    if hg == 0:
                        for c in range(2):
                            self.stt(ckvT.ap[:, c, :], craw.ap[:, c, :], par[:, 54 + c:55 + c], rc[:, :], ALU.mult, ALU.mult, [craw.t[0], PT, rct], [ckvT.t[0]])
                suk, sukt = self.wslice(I["w_uk"][l], 2, 512)
                suv, suvt = self.wslice(I["w_uv"][l], 2, 512)
                for hh in range(8):
                    pk, pkt = self.ps()
                    self.mm(pkt, [(pk[0:64, 0:TW], suk[:, c, hh * 64:(hh + 1) * 64], ckvT.ap[:, c, :]) for c in range(2)], [sukt, ckvT.t[0]])
                    self.cp("act" if hh % 2 else "dve", KT.ap[0:64, hh, kc0:kc0 + TW], pk[0:64, 0:TW], [pkt], [KT.t[hh]])
                for tb in range(nblk):
                    pv, pvt = self.ps()
                    self.mm(pvt, [(pv[0:BW, :], ckvT.ap[:, c, tb * BW:(tb + 1) * BW], suv[:, c, :]) for c in range(2)], [suvt, ckvT.t[0]])
                    gb = (kc0 + tb * BW) // 128
                    self.cp("act", VV.ap[0:BW, gb, :], pv[0:BW, :], [pvt], [VV.t[0]])
                if self.stop == 'B5':
                    continue
                qc0 = (koff + t * TW) // 64
                qc1 = (koff + (t + 1) * TW) // 64 - 1
                pend = []
                for hh in range(8):
                    hb, pr = hh % 2, hh // 2
                    po, pot = self.psacc()
                    pm, pmt = self.psacc()
                    blocks = []
                    for j in range(nkblk):
                        kw = min(128, NK - j * 128)
                        if 2 * j > qc1:
                            continue
                        if 2 * j + 1 <= qc0 or kw <= 64:
                            blocks.append((j, kw, 0, False))
                        else:
                            blocks.append((j, kw, (2 * j - qc0) * 64, True))
                    nb_ = len(blocks)
                    for bi, (j, kw, c0, diag) in enumerate(blocks):
                        psc, psct = self.ps()
                        self.mm(psct, [(psc[0:kw, c0:TW], KT.ap[0:96, hh, j * 128:j * 128 + kw], QT.ap[0:96, hh, c0:TW])], [KT.t[hh], QT.t[hh]])
                        pa_, pt_ = btmp()
                        self.act(pa_[0:kw, c0:TW], psc[0:kw, c0:TW], AF.Exp, [psct], [pt_], scale=MLA_SCALE)
                        if diag:
                            self.memset("dve", pa_[64:128, c0:c0 + 64], 0.0, [pt_])

                        def fin(po=po, pot=pot, pm=pm, pmt=pmt, pa_=pa_, pt_=pt_, j=j, kw=kw, c0=c0, bi=bi, nb_=nb_, pr=pr, hb=hb):
                            self.mm_acc(pot, po[:, c0:TW], VV.ap[0:kw, j, pr * 128:(pr + 1) * 128], pa_[0:kw, c0:TW], bi == 0, bi == nb_ - 1, [VV.t[0], pt_])
                            self.mm_acc(pmt, pm[:, c0:TW], ones[0:kw, :], pa_[0:kw, c0:TW], bi == 0, bi == nb_ - 1, [ONT, pt_])
                            if bi == nb_ - 1:
                                rs = slice(hb * 64, hb * 64 + 64)
                                ra, rt = ftmp()
                                self.R.op("dve", lambda e, ra=ra, pm=pm, rs=rs: e.reciprocal(out=ra[rs, :], in_=pm[rs, 0:TW]), reads=[pmt], writes=[rt])
                                self.tt("dve", oa.ap[rs, pr, tc(t)], po[rs, 0:TW], ra[rs, :], ALU.mult, [pot, rt], [oa.t[t]])
                        pend.append(fin)
                        if len(pend) > 3:
                            pend.pop(0)()
                while pend:
                    pend.pop(0)()
            R.barrier()
            if self.stop is not None and self.stop.startswith('B'):
                break

            build_mm = l not in self.mm_built
            use(LC1 if build_mm else LC2)
            Mm = cur["Mm"]
            if build_mm:
                bmaskr, ext, bneg = cur["bmaskr"], cur["ext"], cur["bneg"]
            if not build_mm:
                mmT = self.mm_built[l]
                self.R.dma("sp", lambda e, l=l, Mm=Mm: e.dma_start(out=Mm.ap[:, :, :], in_=self.scr_mm[l].rearrange("p (h n) -> p h n", h=HB)),
                           reads=[mmT], writes=list(Mm.t), semtile=Mm.t[0], par=False)
            if build_mm:
                self.load("pool", bneg.ap[:, :], I["c_bneg"], bneg.t[0])
                self.load("pool", bmaskr.ap[:, :], I["c_bmask"], bmaskr.t[0])
                self.memset("dve", ext.ap[:, :], 0.0, [ext.t[0]])
                self.load("sp", ext.ap[:, 383:640], I["relb"][l], ext.t[0], par=False)
                self.R.op("dve", lambda e: e.tensor_scalar(out=ext.ap[:, 0:383], in0=ext.ap[:, 0:383], scalar1=ext.ap[:, 383:384], scalar2=None, op0=ALU.add),
                          reads=[ext.t[0]], writes=[ext.t[0]])
                self.R.op("dve", lambda e: e.tensor_scalar(out=ext.ap[:, 640:1536], in0=ext.ap[:, 640:1536], scalar1=ext.ap[:, 639:640], scalar2=None, op0=ALU.add),
                          reads=[ext.t[0]], writes=[ext.t[0]])
                scrT = getattr(self, "scrT", None)
                if scrT is None:
                    scrT = self.scrT = T("scr")
                self.R.dma("sp", lambda e: e.dma_start(out=self.scr, in_=ext.ap[:, :]), reads=[ext.t[0]], writes=[scrT], semtile=scrT, par=False)
            jrev, JT = self.jrev.ap, self.jrev.t[0]
            for hh in (range(8) if build_mm else ()):
                mraw = cur["mraw%d" % (hh % 4)]
                mrev = cur["mrev%d" % (hh % 2)]
                src = bass.AP(tensor=self.scr.tensor, offset=hh * 1536, ap=[[1, 128], [1, MW]])
                self.R.dma("sp", lambda e, src=src, mraw=mraw: e.dma_start(out=mraw.ap[:, :], in_=src), reads=[scrT], writes=[mraw.t[0]], semtile=mraw.t[0], par=False)
                self.tt("dve", mraw.ap[:, :], mraw.ap[:, :], bmaskr.ap[:, :], ALU.mult, [mraw.t[0], bmaskr.t[0]], [mraw.t[0]])
                self.stt(mrev.ap[:, :], mraw.ap[:, :], 1.0 / BAND_SCALE, bneg.ap[:, :], ALU.mult, ALU.add, [mraw.t[0], bneg.t[0]], [mrev.t[0]])
                for c3 in range(3):
                    w_ = 512 if c3 < 2 else MW - 1024
                    pj, pjt = self.ps()
                    self.mm(pjt, [(pj[:, 0:w_], jrev[:, :], mrev.ap[:, c3 * 512:c3 * 512 + w_])], [JT, mrev.t[0]])
                    self.cp("act", Mm.ap[:, hh, c3 * 512:c3 * 512 + w_], pj[:, 0:w_], [pjt], [Mm.t[hh]])
            if build_mm:
                mmT = self.mm_built[l] = T("scrmm%d" % l)
                self.R.dma("sp", lambda e, l=l, Mm=Mm: e.dma_start(out=self.scr_mm[l].rearrange("p (h n) -> p h n", h=HB), in_=Mm.ap[:, :, :]),
                           reads=list(Mm.t), writes=[mmT], semtile=mmT, par=False)
            if build_mm:
                R.barrier()
            if self.stop == 'C1':
                break
            use(LC2)
            xb, QbT, KbT, Vb = cur["xb"], cur["QbT"], cur["KbT"], cur["Vb"]
            for t in range(ntile):
                cast_xb(t, 0)
                xs = xb.ap[:, :, 0:TW]
                XBT = xb.t[0]
                kbase = (t % 2) * 512 if not sample else 512
                sq, sqt = self.wslice(win[:, 1056:1568], 8, 512)
                sk, skt = self.wslice(win[:, 1568:2080], 8, 512)
                sv, svt = self.wslice(win[:, 2080:2592], 8, 512)
                for pr in range(4):
                    pq, pqt = self.ps()
                    self.mm(pqt, [(pq[:, 0:TW], sq[:, k, pr * 128:(pr + 1) * 128], xs[:, k, :]) for k in range(8)], [sqt, XBT])
                    self.cp("act", QbT.ap[:, pr, :], pq[:, 0:TW], [pqt], [QbT.t[0]])
                    pk, pkt = self.ps()
                    self.mm(pkt, [(pk[:, 0:TW], sk[:, k, pr * 128:(pr + 1) * 128], xs[:, k, :]) for k in range(8)], [skt, XBT])
                    self.cp("dve", KbT.ap[:, pr, kbase:kbase + TW], pk[:, 0:TW], [pkt], [KbT.t[0]])
                want_out = sample or (t == ntile - 1)
                for tb in range(nblk):
                    pv, pvt = self.ps()
                    self.mm(pvt, [(pv[0:BW, :], xs[:, k, tb * BW:(tb + 1) * BW], sv[:, k, :]) for k in range(8)], [svt, XBT])
                    vblk = (kbase + tb * BW) // 128
                    self.cp("act", Vb.ap[0:BW, vblk, :], pv[0:BW, :], [pvt], [Vb.t[0]])
                    if want_out:
                        sa, st_ = stage()
                        self.cp("dve", sa[0:BW, 0:512], pv[0:BW, :], [pvt], [st_])
                        pk, pkt = self.ps()
                        self.mm(pkt, [(pk[0:BW, :], xs[:, k, tb * BW:(tb + 1) * BW], sk[:, k, :]) for k in range(8)], [skt, XBT])
                        self.cp("dve", sa[0:BW, 512:1024], pk[0:BW, :], [pkt], [st_])
                        r0 = tb * BW
                        ov = O["vb_s"][l, 0] if sample else O["vb_p"][l, sidx]
                        okb = O["kb_s"][l, 0] if sample else O["kb_p"][l, sidx]
                        self.store(ov[r0:r0 + BW, :], sa[0:BW, 0:512], st_)
                        self.store(okb[r0:r0 + BW, :], sa[0:BW, 512:1024], st_)
                if self.stop == 'C2':
                    continue
                pend = []
                for hh in range(8):
                    hb, pr = hh % 2, hh // 2
                    rs = slice(hb * 64, hb * 64 + 64)
                    po, pot = self.psacc()
                    pm, pmt = self.psacc()
                    blocks = []
                    for r in (0, -1, -2, -3, -4, 1, 2, 3):
                        if sample:
                            if r > 0:
                                continue
                            kcol = 512 + 128 * r
                            kw = 64 if r == 0 else 128
                            qlo, qhi = 0, 0
                        else:
                            gb = 4 * t + r
                            if gb < 0:
                                continue
                            kcol = (gb * 128) % 1024
                            kw = 128
                            qlo, qhi = max(0, 2 * r), min(7, 2 * r + 9)
                        blocks.append((r, kcol, kw, qlo * 64, min(TW, (qhi + 1) * 64)))
                    nb_ = len(blocks)
                    for bi, (r, kcol, kw, a0, a1) in enumerate(blocks):
                        psc, psct = self.ps()
                        m0 = a0 - 128 * r + 384
                        self.mm(psct, [(psc[0:kw, a0:a1], KbT.ap[rs, pr, kcol:kcol + kw], QbT.ap[rs, pr, a0:a1]),
                                       (psc[0:kw, a0:a1], self.identb.ap[:, 0:kw], Mm.ap[:, hh, m0:m0 + (a1 - a0)])],
                                [KbT.t[0], QbT.t[0], Mm.t[hh], self.identb.t[0]])
                        pa_, pt_ = btmp()
                        self.act(pa_[0:kw, a0:a1], psc[0:kw, a0:a1], AF.Exp, [psct], [pt_], scale=BAND_SCALE)
                        vblk = kcol // 128

                        def fin(po=po, pot=pot, pm=pm, pmt=pmt, pa_=pa_, pt_=pt_, vblk=vblk, kw=kw, a0=a0, a1=a1, bi=bi, nb_=nb_, pr=pr, rs=rs):
                            self.mm_acc(pot, po[:, a0:a1], Vb.ap[0:kw, vblk, pr * 128:(pr + 1) * 128], pa_[0:kw, a0:a1], bi == 0, bi == nb_ - 1, [Vb.t[0], pt_])
                            self.mm_acc(pmt, pm[:, a0:a1], ones[0:kw, :], pa_[0:kw, a0:a1], bi == 0, bi == nb_ - 1, [ONT, pt_])
                            if bi == nb_ - 1:
                                ra, rt = ftmp()
                                self.R.op("dve", lambda e, ra=ra, pm=pm, rs=rs: e.reciprocal(out=ra[rs, :], in_=pm[rs, 0:TW]), reads=[pmt], writes=[rt])
                                self.tt("dve", ob.ap[rs, pr, tc(t)], po[rs, 0:TW], ra[rs, :], ALU.mult, [pot, rt], [ob.t[t]])
                        pend.append(fin)
                        if len(pend) > 3:
                            pend.pop(0)()
                while pend:
                    pend.pop(0)()
            R.barrier()
            if self.stop in ('C', 'C2'):
                break

            use(LD)
            xb, mix = cur["xb"], cur["mix"]
            big = [(cur["bigA"].ap, cur["bigA"].t[0]), (cur["bigB"].ap, cur["bigB"].t[0])]
            smr = [(cur["smA"].ap, cur["smA"].t[0]), (cur["smB"].ap, cur["smB"].t[0])]
            for t in range(ntile):
                cast_xb(t, t)
            for mp in range(4):
                gbuf, gt = big[mp % 2]
                gv = gbuf[:, 0:4096].rearrange("p (g k n) -> p g k n", g=2, k=8)
                for g in range(2):
                    c0 = 2592 + g * 1024 + mp * 256
                    self.load("pool", gv[:, g, :, :], win[:, c0:c0 + 256].rearrange("(k p) n -> p k n", p=128), gt, par=(g == 1))
                pbuf, ptl = smr[mp % 2]
                pv_ = pbuf[:, 0:2048].rearrange("p (g k n) -> p g k n", g=2, k=4)
                for g, nm in enumerate(("w_pa", "w_pb")):
                    self.load("pool", pv_[:, g, :, :], I[nm][l][:, mp * 256:(mp + 1) * 256].rearrange("(k p) n -> p k n", p=128), ptl, par=(g == 1))
                for mm_ in range(2):
                    m = mp * 2 + mm_
                    ms = slice(mm_ * 128, (mm_ + 1) * 128)
                    for t in range(ntile):
                        xs = xb.ap[:, :, tc(t)]
                        XBT = xb.t[t]
                        pga, pgat = self.ps()
                        pgb, pgbt = self.ps()
                        pA, pAt = self.psacc()
                        pB, pBt = self.psacc()
                        self.mm(pgat, [(pga[:, 0:TW], gv[:, 0, k, ms], xs[:, k, :]) for k in range(8)], [gt, XBT])
                        self.mm(pgbt, [(pgb[:, 0:TW], gv[:, 1, k, ms], xs[:, k, :]) for k in range(8)], [gt, XBT])
                        self.mm(pAt, [(pA[:, 0:TW], pv_[:, 0, k, ms], oa.ap[:, k, tc(t)]) for k in range(4)], [ptl, oa.t[t]])
                        self.mm(pBt, [(pB[:, 0:TW], pv_[:, 1, k, ms], ob.ap[:, k, tc(t)]) for k in range(4)], [ptl, ob.t[t]])
                        ga_, gat_ = ftmp()
                        gb_, gbt_ = ftmp()
                        self.act(ga_[:, :], pga[:, 0:TW], AF.Sigmoid, [pgat], [gat_])
                        self.act(gb_[:, :], pgb[:, 0:TW], AF.Sigmoid, [pgbt], [gbt_])
                        self.tt("dve", ga_[:, :], ga_[:, :], pA[:, 0:TW], ALU.mult, [gat_, pAt], [gat_])
                        self.tt("dve", gb_[:, :], gb_[:, :], pB[:, 0:TW], ALU.mult, [gbt_, pBt], [gbt_])
                        self.tt("dve", mix.ap[:, m, tc(t)], ga_[:, :], gb_[:, :], ALU.add, [gat_, gbt_], [mix.t[t]])
            sos = []
            for half in range(2):
                sbuf_, sot = big[half]
                so = sbuf_[:, 0:4096].rearrange("p (k n) -> p k n", k=8)
                self.load("pool", so, I["w_out"][l][:, half * 512:(half + 1) * 512].rearrange("(k p) n -> p k n", p=128), sot, par=False)
                sos.append((so, sot))
            gens = []

            def tickd():
                while gens:
                    try:
                        next(gens[0])
                        return
                    except StopIteration:
                        gens.pop(0)
            for t in range(ntile):
                for m in range(8):
                    so, sot = sos[m // 4]
                    m4 = m % 4
                    po, pot = self.ps()
                    self.mm(pot, [(po[:, 0:TW], so[:, k, m4 * 128:(m4 + 1) * 128], mix.ap[:, k, tc(t)]) for k in range(8)], [sot, mix.t[t]])
                    self.stt(X[:, m, tc(t)], po[:, 0:TW], 1.0 / ALPHA, X[:, m, tc(t)], ALU.mult, ALU.add, [pot, x32.t[t]], [x32.t[t]])
                    tickd()
                    tickd()
                if t < ST:
                    gens.append(layer_norm(t, 16, 24, par, PT))
            drain(gens)
            carry2 = [layer_norm(t, 16, 24, par, PT) for t in range(ST, ntile)]
            R.barrier()
            if self.stop == 'D':
                break

            use(LA)
            carry = list(carry) + carry2
            for s in range((ntile + ST - 1) // ST):
                carry = ffn(s, I["ffn2_w1"][l], I["ffn2_w2"][l], 32, 40, par, PT, inter=carry)

        drain(carry)
        R.barrier()
        use(LA)
        yout = O["y_s"][0] if sample else O["y_p"][sidx]
        for t in range(ntile):
            for tb in range(nblk):
                r0 = t * TW + tb * BW
                sa, st_ = stage()
                for half in range(2):
                    pb, pt = self.ps()
                    fns = []
                    for c4 in range(4):
                        c = half * 4 + c4
                        fns.append(lambda e, o=pb[0:BW, c4 * 128:(c4 + 1) * 128], i_=X[:, c, r0:r0 + BW]: e.transpose(o, i_, ident[:, :]))
                    R.group("pe", fns, reads=[x32.t[t], IDT], writes=[pt])
                    self.cp("act" if half else "dve", sa[0:BW, half * 512:(half + 1) * 512], pb[0:BW, :], [pt], [st_])
                self.store(yout[r0:r0 + BW, :], sa[0:BW, :], st_)


def _consts():
    ident = np.eye(128, dtype=np.float32)
    jrev = np.ascontiguousarray(ident[::-1])
    half = RD // 2
    inv = (np.float32(10000.0) ** (-np.arange(half, dtype=np.float32) / np.float32(half))).astype(np.float32)
    pos = np.concatenate([np.arange(SEQ), PAST + np.arange(DEC_SEQ)]).astype(np.float32)
    ang = (pos[:, None] * inv[None, :]).astype(np.float32)
    cos = np.cos(ang).astype(np.float32)
    sin = np.sin(ang).astype(np.float32)
    ropeT = np.zeros((NPOS, 2, RD), np.float32)
    ropeT[:, 0, :half] = cos
    ropeT[:, 0, half:] = cos
    ropeT[:, 1, :half] = -sin
    ropeT[:, 1, half:] = sin
    ropeF = np.zeros((2, 128, NPOS), np.float32)
    for p in range(128):
        r = p % 32
        ropeF[0, p] = cos[:, r % 16]
        ropeF[1, p] = (-sin[:, r] if r < 16 else sin[:, r - 16])
    kk = 127 - np.arange(128)[:, None]
    c = np.arange(MW)[None, :] - 384
    cq = np.floor_divide(c, 64)
    kq = kk // 64
    bmask = ((kq >= cq - 8) & (kq <= cq)).astype(np.float32)
    bneg = ((bmask - 1.0) * 240000.0).astype(np.float32)
    return dict(c_ident=ident, c_jrev=jrev, c_ropeF=ropeF, c_ropeT=ropeT, c_bmask=np.ascontiguousarray(bmask), c_bneg=np.ascontiguousarray(bneg))


_CACHE = {}


def _get_builder(key=(DEPTH, (0, 1), True)):
    if key not in _CACHE:
        _CACHE[key] = Builder(n_layers=key[0], prompt_seqs=key[1], do_sample=key[2], stop=(key[3] if len(key) > 3 else None))
    return _CACHE[key]


def kernel(x_prompt, x_sample, cache_mla_ckv, cache_mla_krope, cache_band_k, cache_band_v,
           ln1_g, ln1_b, ffn1_w1, ffn1_w2, w_in, mla_q_norm_g, mla_w_uq, mla_kv_norm_g,
           mla_w_uk, mla_w_uv, band_rel_bias, w_proj_a, w_proj_b, w_out,
           ln2_g, ln2_b, ffn2_w1, ffn2_w2, ln3_g, ln3_b, _dbg=None, _ncores=8, _trace=False):
    f = lambda a: np.ascontiguousarray(np.asarray(a, dtype=np.float32))
    key = _dbg if _dbg is not None else (DEPTH, (0, 1), True)
    B = _get_builder(key)
    consts = _consts()
    shared = dict(ln1_g=f(ln1_g), ln1_b=f(ln1_b), ffn1_w1=f(ffn1_w1), ffn1_w2=f(ffn1_w2), w_in=f(w_in), qg=f(mla_q_norm_g),
                  w_uq=f(mla_w_uq), kvg=f(mla_kv_norm_g), w_uk=f(mla_w_uk), w_uv=f(mla_w_uv), relb=f(band_rel_bias),
                  w_pa=f(w_proj_a), w_pb=f(w_proj_b), w_out=f(w_out), ln2_g=f(ln2_g), ln2_b=f(ln2_b),
                  ffn2_w1=f(ffn2_w1), ffn2_w2=f(ffn2_w2), ln3_g=f(ln3_g), ln3_b=f(ln3_b), **consts)
    xp, xs = f(x_prompt), f(x_sample)
    cc, ck = f(cache_mla_ckv), f(cache_mla_krope)
    cbk, cbv = f(cache_band_k).reshape(DEPTH, 8, 512, 512), f(cache_band_v).reshape(DEPTH, 8, 512, 512)
    in_maps = []
    for c in range(_ncores):
        m = dict(shared)
        m["x_prompt"] = np.ascontiguousarray(xp[2 * c:2 * c + 2])
        m["x_sample"] = np.ascontiguousarray(xs[c:c + 1])
        m["cache_ckv"] = np.ascontiguousarray(cc[:, c])
        m["cache_kr"] = np.ascontiguousarray(ck[:, c])
        m["cache_bk"] = np.ascontiguousarray(cbk[:, c])
        m["cache_bv"] = np.ascontiguousarray(cbv[:, c])
        in_maps.append(m)
    if _trace:
        res = run_bass_kernel_spmd(B.nc, in_maps, core_ids=list(range(_ncores)), trace=True)
        print('EXEC_TIME_NS', res.exec_time_ns)
        return res.results
    res = run_bass_kernel_spmd(B.nc, in_maps, core_ids=list(range(_ncores)))
    r = res.results
    if _ncores < 8:
        return r
    cat = lambda k, ax: np.concatenate([r[c][k] for c in range(8)], axis=ax)
    y_p = cat("y_p", 0)
    y_s = cat("y_s", 0)
    ckv_p = cat("ckv_p", 1)
    kr_p = cat("kr_p", 1)
    kb_p = cat("kb_p", 1).reshape(DEPTH, 16, 512, HB, DB)
    vb_p = cat("vb_p", 1).reshape(DEPTH, 16, 512, HB, DB)
    ckv_s = cat("ckv_s", 1)
    kr_s = cat("kr_s", 1)
    kb_s = cat("kb_s", 1).reshape(DEPTH, 8, DEC_SEQ, HB, DB)
    vb_s = cat("vb_s", 1).reshape(DEPTH, 8, DEC_SEQ, HB, DB)
    return (y_p, y_s, ckv_p, kr_p, kb_p, vb_p, ckv_s, kr_s, kb_s, vb_s)
```
